# Optimizing a Trainium2 kernel written in Bass

```python
import jax, jax.numpy as jnp
from jax import lax
import numpy as np

D_MODEL = 1024
BATCH = 2
SEQ = 8192
DEPTH = 4
DEC_BATCH = 16
DEC_SEQ = 4096
PAST_LEN = 128

POOL_WIDTH = D_MODEL // 2
POOL_GROUPS = 4
POOL_GC = POOL_WIDTH // POOL_GROUPS
POOL_WINDOWS = (2, 4, 8, 16)
SGU_WIDTH = D_MODEL // 2
SGU_GROUPS = 4
SGU_GC = SGU_WIDTH // SGU_GROUPS
SGU_CHUNK = 128
HEAD_DIM = 64
ATTN_GROUPS = ((128, 1), (512, 4), (2048, 16))
HEADS_PER_GROUP = D_MODEL // 256
N_ATTN_HEADS = HEADS_PER_GROUP * len(ATTN_GROUPS)
ATTN_WIDTH = N_ATTN_HEADS * HEAD_DIM
ATTN_OUT = HEADS_PER_GROUP * HEAD_DIM
N_BUCKETS = 32
T5_MAX_DIST = 1024
N_BRANCH = 3
EPS = 1e-6
NEG_INF = -1e30
IN_SIZES = (POOL_WIDTH, POOL_WIDTH,
            SGU_WIDTH, SGU_WIDTH, SGU_WIDTH,
            ATTN_WIDTH, ATTN_WIDTH, ATTN_WIDTH,
            ATTN_OUT,
            N_BRANCH * D_MODEL)
N_IN = sum(IN_SIZES)

kernel_name = "hybrid_pool_sgu_dilated_encoder"


def _rmsnorm(x, g):
    xf = x.astype(jnp.float32)
    y = xf * lax.rsqrt(jnp.mean(xf * xf, axis=-1, keepdims=True) + EPS)
    return (y * g.astype(jnp.float32)).astype(x.dtype)


def _layernorm(x, g, b):
    xf = x.astype(jnp.float32)
    mu = jnp.mean(xf, axis=-1, keepdims=True)
    xc = xf - mu
    var = jnp.mean(xc * xc, axis=-1, keepdims=True)
    y = xc * lax.rsqrt(var + EPS) * g.astype(jnp.float32) + b.astype(jnp.float32)
    return y.astype(x.dtype)


def _t5_bucket(rel):
    half = N_BUCKETS // 2
    n = -rel
    ret = (n < 0).astype(np.int32) * half
    n = np.abs(n)
    max_exact = half // 2
    large = max_exact + (np.log(np.maximum(n, 1) / max_exact) / np.log(T5_MAX_DIST / max_exact)
                         * (half - max_exact)).astype(np.int32)
    large = np.minimum(large, half - 1)
    return (ret + np.where(n < max_exact, n, large)).astype(np.int32)


def _pool_mixer(xa, pool_w, pool_scale):
    B, S, _ = xa.shape
    xf = xa.reshape(B, S, POOL_GROUPS, POOL_GC).astype(jnp.float32)
    cs = jnp.concatenate([jnp.zeros((B, 1, POOL_GROUPS, POOL_GC), jnp.float32),
                          jnp.cumsum(xf, axis=1)], axis=1)
    t = jnp.arange(S, dtype=jnp.int32)
    pooled = []
    for gi, w in enumerate(POOL_WINDOWS):
        lo = jnp.clip(t - w // 2, 0, S - 1)
        hi = jnp.clip(t + w // 2 - 1, 0, S - 1)
        csg = cs[:, :, gi]
        cnt = (hi - lo + 1).astype(jnp.float32)
        pooled.append((csg[:, hi + 1] - csg[:, lo]) / cnt[None, :, None])
    mixed = (jnp.stack(pooled, axis=2) - xf).astype(xa.dtype)
    y = jnp.einsum('bsgc,gcd->bsgd', mixed, pool_w)
    return y.reshape(B, S, POOL_WIDTH) * pool_scale


def _sgu_mixer(u, v, ln_g, ln_b, w_s, b_s):
    B, S, _ = v.shape
    vn = _layernorm(v, ln_g, ln_b).reshape(B, S // SGU_CHUNK, SGU_CHUNK, SGU_GROUPS, SGU_GC)
    sp = jnp.einsum('gpq,bnqgc->bnpgc', w_s, vn) + jnp.transpose(b_s)[:, :, None]
    return u * sp.reshape(B, S, SGU_WIDTH)


def _dilated_group(q, k, v, table_g, window, dil):
    B, S, H, Dh = q.shape
    half = window // (2 * dil)
    blk = half
    L = S // dil
    Lp = -(-L // blk) * blk
    nb = Lp // blk
    N = B * dil

    def to_sub(t):
        return t.reshape(B, L, dil, H, Dh).transpose(0, 2, 1, 3, 4).reshape(N, L, H, Dh)

    def band(t):
        tp = jnp.pad(t, ((0, 0), (blk, Lp - L + blk), (0, 0), (0, 0))).reshape(N, nb + 2, blk, H, Dh)
        return jnp.concatenate([tp[:, :-2], tp[:, 1:-1], tp[:, 2:]], axis=2)

    qb = jnp.pad(to_sub(q), ((0, 0), (0, Lp - L), (0, 0), (0, 0))).reshape(N, nb, blk, H, Dh)
    kw = band(to_sub(k))
    vw = band(to_sub(v))
    rel = np.arange(3 * blk)[None, :] - blk - np.arange(blk)[:, None]
    bias = jnp.transpose(table_g[_t5_bucket(rel * dil)], (2, 0, 1)).astype(jnp.float32)
    keypos = np.arange(nb)[:, None] * blk - blk + np.arange(3 * blk)[None, :]
    valid = (np.abs(rel) <= half)[None] & ((keypos >= 0) & (keypos < L))[:, None, :]
    s = jnp.einsum('nbqhd,nbkhd->nbhqk', qb, kw).astype(jnp.float32) * (Dh ** -0.5) + bias
    s = jnp.where(valid[None, :, None], s, NEG_INF)
    m = jnp.max(s, axis=-1, keepdims=True)
    p = jnp.exp(s - m)
    den = jnp.sum(p, axis=-1)
    o = jnp.einsum('nbhqk,nbkhd->nbqhd', p.astype(vw.dtype), vw).astype(jnp.float32)
    o = o / jnp.swapaxes(den, 2, 3)[..., None]
    lse = jnp.swapaxes(m[..., 0] + jnp.log(den), 2, 3)
    o = o.reshape(N, Lp, H, Dh)[:, :L].reshape(B, dil, L, H, Dh).transpose(0, 2, 1, 3, 4).reshape(B, S, H, Dh)
    lse = lse.reshape(N, Lp, H)[:, :L].reshape(B, dil, L, H).transpose(0, 2, 1, 3).reshape(B, S, H)
    return o, lse


def _dilated_mixer(q, k, v, rel_bias):
    B, S, _ = q.shape
    q = q.reshape(B, S, N_ATTN_HEADS, HEAD_DIM)
    k = k.reshape(B, S, N_ATTN_HEADS, HEAD_DIM)
    v = v.reshape(B, S, N_ATTN_HEADS, HEAD_DIM)
    outs, lses = [], []
    for gi, (window, dil) in enumerate(ATTN_GROUPS):
        sl = slice(gi * HEADS_PER_GROUP, (gi + 1) * HEADS_PER_GROUP)
        o, l = _dilated_group(q[:, :, sl], k[:, :, sl], v[:, :, sl], rel_bias[:, sl], window, dil)
        outs.append(o)
        lses.append(l)
    wts = jax.nn.softmax(jnp.stack(lses, axis=0), axis=0)
    out = jnp.sum(wts[..., None] * jnp.stack(outs, axis=0), axis=0)
    return out.reshape(B, S, ATTN_OUT).astype(q.dtype)


def _encoder(x, norm_g, w_in, pool_w, pool_scale, sgu_ln_g, sgu_ln_b, sgu_w, sgu_b,
             rel_bias, w_br_a, w_br_b, w_br_c, w_out, final_g):
    split_at = [int(c) for c in np.cumsum(IN_SIZES)[:-1]]
    for l in range(DEPTH):
        h = _rmsnorm(x, norm_g[l])
        z = jnp.einsum('bsd,de->bse', h, w_in[l])
        (xa, ga, u, vv, gb, q, k, v, gc, mg) = jnp.split(z, split_at, axis=-1)
        a_out = _pool_mixer(xa, pool_w[l], pool_scale[l]) * jax.nn.silu(ga)
        b_out = _sgu_mixer(u, vv, sgu_ln_g[l], sgu_ln_b[l], sgu_w[l], sgu_b[l]) * jax.nn.silu(gb)
        c_out = _dilated_mixer(q, k, v, rel_bias) * jax.nn.silu(gc)
        gate_a, gate_b, gate_c = jnp.split(jax.nn.sigmoid(mg), N_BRANCH, axis=-1)
        merged = (gate_a * jnp.einsum('bsc,cd->bsd', a_out, w_br_a[l])
                  + gate_b * jnp.einsum('bsc,cd->bsd', b_out, w_br_b[l])
                  + gate_c * jnp.einsum('bsc,cd->bsd', c_out, w_br_c[l]))
        x = x + jnp.einsum('bsd,de->bse', merged, w_out[l])
    return _rmsnorm(x, final_g)


def setup_inputs(seed: int = 0) -> dict:
    key = jax.random.key(seed)
    ks = jax.random.split(key, 16)

    def nrm(k, shape, scale):
        return jax.random.normal(k, shape, jnp.float32) * scale

    return {
        'x_prompt': nrm(ks[0], (BATCH, SEQ, D_MODEL), 1.0),
        'x_sample': nrm(ks[1], (DEC_BATCH, DEC_SEQ, D_MODEL), 1.0),
        'norm_g': 1.0 + nrm(ks[2], (DEPTH, D_MODEL), 0.1),
        'w_in': nrm(ks[3], (DEPTH, D_MODEL, N_IN), D_MODEL ** -0.5),
        'pool_w': nrm(ks[4], (DEPTH, POOL_GROUPS, POOL_GC, POOL_GC), POOL_GC ** -0.5),
        'pool_scale': 1.0 + nrm(ks[5], (DEPTH, POOL_WIDTH), 0.1),
        'sgu_ln_g': 1.0 + nrm(ks[6], (DEPTH, SGU_WIDTH), 0.1),
        'sgu_ln_b': nrm(ks[7], (DEPTH, SGU_WIDTH), 0.1),
        'sgu_w': nrm(ks[8], (DEPTH, SGU_GROUPS, SGU_CHUNK, SGU_CHUNK), SGU_CHUNK ** -0.5),
        'sgu_b': 1.0 + nrm(ks[9], (DEPTH, SGU_GROUPS, SGU_CHUNK), 0.1),
        'rel_bias': nrm(ks[10], (N_BUCKETS, N_ATTN_HEADS), 0.5),
        'w_br_a': nrm(ks[11], (DEPTH, POOL_WIDTH, D_MODEL), POOL_WIDTH ** -0.5),
        'w_br_b': nrm(ks[12], (DEPTH, SGU_WIDTH, D_MODEL), SGU_WIDTH ** -0.5),
        'w_br_c': nrm(ks[13], (DEPTH, ATTN_OUT, D_MODEL), ATTN_OUT ** -0.5),
        'w_out': nrm(ks[14], (DEPTH, D_MODEL, D_MODEL), D_MODEL ** -0.5),
        'final_g': 1.0 + nrm(ks[15], (D_MODEL,), 0.1),
    }


def reference(x_prompt, x_sample, norm_g, w_in, pool_w, pool_scale, sgu_ln_g, sgu_ln_b, sgu_w,
              sgu_b, rel_bias, w_br_a, w_br_b, w_br_c, w_out, final_g):
    y_prompt = _encoder(x_prompt, norm_g, w_in, pool_w, pool_scale, sgu_ln_g, sgu_ln_b, sgu_w, sgu_b,
                        rel_bias, w_br_a, w_br_b, w_br_c, w_out, final_g)
    y_sample = _encoder(x_sample, norm_g, w_in, pool_w, pool_scale, sgu_ln_g, sgu_ln_b, sgu_w, sgu_b,
                        rel_bias, w_br_a, w_br_b, w_br_c, w_out, final_g)
    return (y_prompt, y_sample)
```

```python
import numpy as np
import ml_dtypes
from contextlib import ExitStack
import concourse.bass as bass
import concourse.mybir as mybir
from concourse.bass_utils import run_bass_kernel_spmd

F32 = mybir.dt.float32
BF16 = mybir.dt.bfloat16
AF = mybir.ActivationFunctionType
ALU = mybir.AluOpType

D = 1024
NIN = 8192
EPS = 1e-6
POOL_WINDOWS = (2, 4, 8, 16)
DILS = (1, 4, 16)
TT = 512


def _t5_bucket(rel):
    N_BUCKETS, T5_MAX_DIST = 32, 1024
    half = N_BUCKETS // 2
    n = -rel
    ret = (n < 0).astype(np.int32) * half
    n = np.abs(n)
    max_exact = half // 2
    large = max_exact + (np.log(np.maximum(n, 1) / max_exact) / np.log(T5_MAX_DIST / max_exact)
                         * (half - max_exact)).astype(np.int32)
    large = np.minimum(large, half - 1)
    return (ret + np.where(n < max_exact, n, large)).astype(np.int32)


class Buf:
    __slots__ = ("w", "rs", "name")

    def __init__(self, name=""):
        self.w = None
        self.rs = {}
        self.name = name


class Sync:
    def __init__(self, nc):
        self.nc = nc
        self.eng = {"pe": nc.tensor, "act": nc.scalar, "dve": nc.vector, "pool": nc.gpsimd, "sp": nc.sync}
        self.semh = {}
        self.cnt = {}
        self.cur = {}
        self.epoch = 0
        self.waited = {e: {} for e in self.eng}
        self.nwaits = 0
        self.ninst = 0
        self._new_engine_sems()

    def _new_engine_sems(self):
        for e in ("pe", "act", "dve", "pool"):
            s = "E_%s_%d" % (e, self.epoch)
            self.semh[s] = self.nc.alloc_semaphore(name="sem_%s_%d" % (e, self.epoch))
            self.cnt[s] = 0
            self.cur[e] = s

    def new_epoch(self):
        self.barrier()
        self.epoch += 1
        self._new_engine_sems()

    def _deps(self, reads, writes):
        deps = {}
        for b in reads:
            if b.w is not None:
                s, v = b.w
                if deps.get(s, 0) < v:
                    deps[s] = v
        for b in writes:
            if b.w is not None:
                s, v = b.w
                if deps.get(s, 0) < v:
                    deps[s] = v
            for s, v in b.rs.items():
                if deps.get(s, 0) < v:
                    deps[s] = v
        return deps

    def _wait(self, e, deps):
        eng = self.eng[e]
        wd = self.waited[e]
        own_pe = self.cur.get("pe") if e == "pe" else None
        for s, v in deps.items():
            if s == own_pe:
                continue
            if wd.get(s, 0) < v:
                eng.wait_ge(self.semh[s], v)
                wd[s] = v
                self.nwaits += 1

    def _post(self, ev, reads, writes):
        s, v = ev
        for b in writes:
            b.w = ev
            b.rs = {}
        for b in reads:
            if b.rs.get(s, 0) < v:
                b.rs[s] = v

    def op(self, e, fn, reads=(), writes=()):
        self._wait(e, self._deps(reads, writes))
        ins = fn(self.eng[e])
        s = self.cur[e]
        self.cnt[s] += 1
        ins.then_inc(self.semh[s], 1)
        self.ninst += 1
        self._post((s, self.cnt[s]), reads, writes)

    def dma(self, q, key, out, in_, reads=(), writes=(), **kw):
        s = "D_" + key
        if s not in self.semh:
            self.semh[s] = self.nc.alloc_semaphore(name="dsem_" + key)
            self.cnt[s] = 0
        self._wait(q, self._deps(reads, writes))
        ins = self.eng[q].dma_start(out=out, in_=in_, **kw)
        self.cnt[s] += 16
        ins.then_inc(self.semh[s], 16)
        self.ninst += 1
        self._post((s, self.cnt[s]), reads, writes)

    def barrier(self):
        allev = {s: c for s, c in self.cnt.items() if c > 0}
        for e in self.eng:
            self._wait(e, allev)


class Cfg:
    def __init__(self, unit=4096, depth=4, debug=False, phases=("p1", "pa", "p2")):
        self.unit = unit
        self.depth = depth
        self.debug = debug
        self.phases = phases
        self.nt = 3 * unit
        self.ntiles = self.nt // TT
        self.segs = ((0, 2 * unit), (2 * unit, unit))


def build(cfg):
    nc = bass.Bass("TRN2", target_bir_lowering=False)
    S = Sync(nc)
    U, NT, L = cfg.unit, cfg.nt, cfg.depth
    NTILES = cfg.ntiles
    TPU = U // TT

    def din(name, shape, dt=F32):
        return nc.dram_tensor(name, list(shape), dt, kind="ExternalInput").ap()

    def dscr(name, shape, dt):
        kind = "ExternalOutput" if cfg.debug else "Internal"
        return nc.dram_tensor(name, list(shape), dt, kind=kind).ap()

    x_in = din("x", [NT, D])
    w_in = din("w_in", [L, D, NIN])
    w_bra = din("w_br_a", [L, 512, D])
    w_brb = din("w_br_b", [L, 512, D])
    w_brc = din("w_br_c", [L, 256, D])
    w_out = din("w_out", [L, D, D])
    pool_w = din("pool_w", [L, 4, 128, 128])
    sgu_wT = din("sgu_wT", [L, 4, 128, 128])
    norm_g = din("norm_g", [L, D])
    final_g = din("final_g", [1, D])
    pscale_t = din("pscale_t", [L, 128, 4])
    lng_t = din("lng_t", [L, 128, 4])
    lnb = din("sgu_ln_b", [L, 512])
    sgub = din("sgu_b", [L, 512])
    rel_bias = din("rel_bias", [32, 12])
    ident_d = din("ident", [128, 128])
    anti_d = din("anti", [128, 128])
    oh_d = din("oh", [32, 3, 384])
    valid_d = din("valid", [4, 384])
    flag_d = din("flag", [128, 1])
    mL_d = din("mL", [128, 256])
    mR_d = din("mR", [128, 256])
    swA_d = din("swA", [128, 128])
    swB_d = din("swB", [128, 128])
    pinv_d = din("pinv", [6, 4, TT])

    y_d = nc.dram_tensor("y", [NT, D], F32, kind="ExternalOutput").ap()
    xres_d = dscr("xres", [NT, D], F32)
    hT_d = dscr("hT", [8, 128, NT], BF16)
    qT_d = dscr("qT", [6, 128, NT], BF16)
    kT_d = dscr("kT", [6, 128, NT], BF16)
    v_d = dscr("vtok", [NT, 1152], BF16)
    xaT_d = dscr("xaT", [4, 128, NT + 16], BF16)
    sgaT_d = dscr("sgaT", [4, 128, NT], BF16)
    boutT_d = dscr("boutT", [4, 128, NT], BF16)
    sgcT_d = dscr("sgcT", [2, 128, NT], BF16)
    coutT_d = dscr("coutT", [2, 128, NT], BF16)
    u_d = dscr("u_scr", [12, 384], F32)
    E_d = dscr("E_scr", [12, 3, 128, 256], F32)

    dbufs = {}

    def DB(name, t):
        k = (name, t)
        if k not in dbufs:
            dbufs[k] = Buf("%s_%d" % k)
        return dbufs[k]

    def DBr(name, t0, t1):
        return [DB(name, t) for t in range(t0, t1)]

    uid = [0]

    with ExitStack() as gs:
        ps = [gs.enter_context(nc.psum_tensor("ps%d" % i, [128, 512], F32)) for i in range(8)]
        psb = [Buf("ps%d" % i) for i in range(8)]
        ring = [0]

        def psnext(lo=0, hi=8):
            i = lo + (ring[0] % (hi - lo))
            ring[0] += 1
            return ps[i], psb[i]

        def sbt(es, name, shape, dt):
            uid[0] += 1
            t = es.enter_context(nc.sbuf_tensor("%s_%d" % (name, uid[0]), list(shape), dt))
            return t, Buf(name)

        with ExitStack() as es:
            tb, tbB = sbt(es, "tb", [32, 12], F32)
            oh, ohB = sbt(es, "oh", [32, 3, 384], F32)
            val, valB = sbt(es, "val", [4, 384], F32)
            ue, ueB = sbt(es, "ue", [4, 384], F32)
            anti, antiB = sbt(es, "anti", [128, 128], F32)
            mL, mLB = sbt(es, "mL", [128, 256], F32)
            mR, mRB = sbt(es, "mR", [128, 256], F32)
            S.dma("sp", "tb", tb[:], rel_bias, writes=[tbB])
            S.dma("sp", "oh", oh[:], oh_d, writes=[ohB])
            S.dma("sp", "val", val[:], valid_d, writes=[valB])
            S.dma("sp", "anti", anti[:], anti_d, writes=[antiB])
            S.dma("sp", "mL", mL[:], mL_d, writes=[mLB])
            S.dma("sp", "mR", mR[:], mR_d, writes=[mRB])
            zt, ztB = sbt(es, "zt", [128, 4, 8], BF16)
            S.op("dve", lambda e: e.memset(zt[:], 0.0), writes=[ztB])
            S.dma("sp", "zt", xaT_d[:, :, 0:8].rearrange("k p n -> p k n"), zt[:], reads=[ztB], writes=[DB("xa", 0)])
            S.dma("sp", "zt", xaT_d[:, :, NT + 8:NT + 16].rearrange("k p n -> p k n"), zt[:], reads=[ztB], writes=[DB("xa", NTILES - 1)])
            uB = Buf("u_d")
            for g in range(3):
                pt, pB = psnext()
                S.op("pe", lambda e, g=g, pt=pt: e.matmul(pt[0:4, 0:384], lhsT=tb[:, 4 * g:4 * g + 4], rhs=oh[:, g, :],
                                                       start=True, stop=True), reads=[tbB, ohB], writes=[pB])
                S.op("act", lambda e, pt=pt: e.activation(out=ue[:], in_=pt[0:4, 0:384], func=AF.Exp),
                     reads=[pB], writes=[ueB])
                S.op("dve", lambda e: e.tensor_tensor(out=ue[:], in0=ue[:], in1=val[:], op=ALU.mult),
                     reads=[ueB, valB], writes=[ueB])
                S.dma("sp", "ue", u_d[4 * g:4 * g + 4, :], ue[:], reads=[ueB], writes=[uB])
            EdB = Buf("E_d")
            hkl = [sbt(es, "hk", [128, 256], F32) for _ in range(2)]
            evl = [sbt(es, "ev", [128, 3, 256], F32) for _ in range(2)]
            for h in range(12):
                hk, hkB = hkl[h % 2]
                ev, evB = evl[h % 2]
                S.dma("sp", "hk%d" % (h % 2), hk[:], bass.AP(u_d.tensor, h * 384, [[1, 128], [1, 256]]),
                      reads=[uB], writes=[hkB])
                pt, pB = psnext()
                S.op("pe", lambda e, pt=pt, hk=hk: e.matmul(pt[:, 0:256], lhsT=anti[:], rhs=hk[:], start=True, stop=True),
                     reads=[antiB, hkB], writes=[pB])
                S.op("act", lambda e, pt=pt, ev=ev: e.copy(out=ev[:, 0, :], in_=pt[:, 0:256]), reads=[pB], writes=[evB])
                S.op("dve", lambda e, ev=ev: e.tensor_tensor(out=ev[:, 1, :], in0=ev[:, 0, :], in1=mL[:], op=ALU.mult),
                     reads=[evB, mLB], writes=[evB])
                S.op("dve", lambda e, ev=ev: e.tensor_tensor(out=ev[:, 2, :], in0=ev[:, 0, :], in1=mR[:], op=ALU.mult),
                     reads=[evB, mRB], writes=[evB])
                S.dma("sp", "ev%d" % (h % 2), E_d[h].rearrange("v p c -> p v c"), ev[:], reads=[evB], writes=[EdB])
            S.barrier()

        for l in range(L):
            if l > 0:
                S.new_epoch()
            x_src = x_in if l == 0 else xres_d
            last = (l == L - 1)

            if "p1" in cfg.phases:
                with ExitStack() as es:
                    W1, W1B = sbt(es, "W1", [128, 8, 5120], BF16)
                    wsT, wsTB = sbt(es, "wsT", [128, 4, 128], BF16)
                    ident_f, identfB = sbt(es, "identf", [128, 128], F32)
                    ident, identB = sbt(es, "ident", [128, 128], BF16)
                    gbc, gbcB = sbt(es, "gbc", [128, D], F32)
                    lnbf, lnbfB = sbt(es, "lnbf", [128, 512], F32)
                    lnbb, lnbbB = sbt(es, "lnbb", [128, 512], BF16)
                    sgr, sgrB = sbt(es, "sgr", [1, 512], BF16)
                    ones1, ones1B = sbt(es, "ones1", [1, 128], BF16)
                    lng, lngB = sbt(es, "lng", [128, 4], F32)
                    Rt, RtB = sbt(es, "Rt", [128, 4, TT], F32)
                    W1G = {j: j for j in range(10)}
                    W1Bs = [Buf("W1g%d" % i) for i in range(10)]

                    def W1R(col):
                        return W1Bs[W1G[col // 512]]
                    def w1_load(blocks):
                        for j in blocks:
                            S.dma("pool", "W1g%d" % W1G[j], W1[:, :, j * 512:(j + 1) * 512],
                                  w_in[l, :, j * 512:(j + 1) * 512].rearrange("(k p) n -> p k n", p=128), writes=[W1Bs[W1G[j]]])
                    w1_load((3, 2, 4))
                    S.dma("pool", "wsT", wsT[:], sgu_wT[l].rearrange("g q p -> q g p"), writes=[wsTB])
                    S.dma("pool", "sgr", sgr[:], sgub[l:l + 1, :], writes=[sgrB])
                    S.dma("sp", "identf", ident_f[:], ident_d, writes=[identfB])
                    S.dma("sp", "gbc", gbc[:], norm_g[l:l + 1, :].partition_broadcast(128), writes=[gbcB])
                    S.dma("sp", "lnbf", lnbf[:], lnb[l:l + 1, :].partition_broadcast(128), writes=[lnbfB])
                    S.dma("sp", "lng", lng[:], lng_t[l], writes=[lngB])
                    S.op("dve", lambda e: e.tensor_copy(out=ident[:], in_=ident_f[:]), reads=[identfB], writes=[identB])
                    S.op("dve", lambda e: e.tensor_copy(out=lnbb[:], in_=lnbf[:]), reads=[lnbfB], writes=[lnbbB])
                    S.op("dve", lambda e: e.memset(ones1[:], 1.0), writes=[ones1B])
                    for g in range(4):
                        pt, pB = psnext()
                        S.op("pe", lambda e, g=g, pt=pt: e.matmul(pt[:, 0:128], lhsT=lnbb[:, g * 128:(g + 1) * 128],
                                                               rhs=wsT[:, g, :], start=True, stop=False),
                             reads=[lnbbB, wsTB], writes=[pB])
                        S.op("pe", lambda e, g=g, pt=pt: e.matmul(pt[:, 0:128], lhsT=ones1[:, :],
                                                               rhs=sgr[:, g * 128:(g + 1) * 128], start=False, stop=True),
                             reads=[ones1B, sgrB], writes=[pB])
                        for c in range(4):
                            S.op("act", lambda e, g=g, pt=pt, c=c: e.copy(out=Rt[:, g, c * 128:(c + 1) * 128], in_=pt[:, 0:128]),
                                 reads=[pB], writes=[RtB])

                    xs = [sbt(es, "xs", [128, D], F32) for _ in range(4)]
                    xn = [sbt(es, "xn", [128, D], BF16) for _ in range(4)]
                    junk, junkB = sbt(es, "junk", [128, D], BF16)
                    ssq = [sbt(es, "ssq", [128, 2], F32) for _ in range(4)]
                    hT = [sbt(es, "hT", [128, 8, TT], BF16) for _ in range(2)]
                    xa_o, xa_oB = sbt(es, "xa_o", [128, 4, TT], BF16)
                    sga_o, sga_oB = sbt(es, "sga_o", [128, 4, TT], BF16)
                    sgc_o, sgc_oB = sbt(es, "sgc_o", [128, 2, TT], BF16)
                    q_o, q_oB = sbt(es, "q_o", [128, 6, TT], BF16)
                    k_o, k_oB = sbt(es, "k_o", [128, 6, TT], BF16)
                    v_o, v_oB = sbt(es, "v_o", [128, 4, 6, 192], BF16)
                    S.op("pool", lambda e: e.memset(v_o[:], 1.0), writes=[v_oB])
                    u_s, u_sB = sbt(es, "u_s", [128, 4, TT], F32)
                    sgb_s, sgb_sB = sbt(es, "sgb_s", [128, 4, TT], F32)
                    vn = [sbt(es, "vn", [128, 512], BF16) for _ in range(4)]
                    bst = [sbt(es, "bst", [128, 8], F32) for _ in range(4)]
                    t1 = [sbt(es, "t1", [128, TT], F32) for _ in range(2)]
                    t2 = [sbt(es, "t2", [128, TT], F32) for _ in range(2)]
                    bo_o, bo_oB = sbt(es, "bo_o", [128, 4, TT], BF16)

                    def p1_load(t):
                        for c in range(4):
                            xt, xB = xs[c]
                            r0 = t * TT + c * 128
                            S.dma("sp", "xs%d" % c, xt[:], x_src[r0:r0 + 128, :], reads=[DB("x", t)], writes=[xB])

                    evac_rr = [0]

                    def evac_copy(out, in_, reads, writes):
                        evac_rr[0] += 1
                        if evac_rr[0] % 2:
                            S.op("act", lambda e: e.copy(out=out, in_=in_), reads=reads, writes=writes)
                        else:
                            S.op("dve", lambda e: e.tensor_copy(out=out, in_=in_), reads=reads, writes=writes)

                    def p1_norm_elem(t):
                        for c in range(4):
                            xt, xB = xs[c]
                            sq, sqB = ssq[c]
                            xnt, xnB = xn[c]
                            S.op("act", lambda e, xt=xt, sq=sq: e.activation(out=junk[:], in_=xt[:], func=AF.Square,
                                                                             accum_out=sq[:, 0:1]),
                                 reads=[xB], writes=[junkB, sqB])
                            S.op("dve", lambda e, sq=sq: e.tensor_scalar(out=sq[:, 1:2], in0=sq[:, 0:1], scalar1=1.0 / D,
                                                                         scalar2=EPS, op0=ALU.mult, op1=ALU.add),
                                 reads=[sqB], writes=[sqB])
                            S.op("act", lambda e, sq=sq: e.sqrt(out=sq[:, 1:2], in_=sq[:, 1:2]), reads=[sqB], writes=[sqB])
                            S.op("dve", lambda e, sq=sq: e.reciprocal(out=sq[:, 1:2], in_=sq[:, 1:2]), reads=[sqB], writes=[sqB])
                            S.op("dve", lambda e, xt=xt, sq=sq, xnt=xnt: e.scalar_tensor_tensor(
                                out=xnt[:], in0=xt[:], scalar=sq[:, 1:2], in1=gbc[:], op0=ALU.mult, op1=ALU.mult),
                                reads=[xB, sqB, gbcB], writes=[xnB])

                    def p1_norm_pe(t):
                        hTt, hTB = hT[t % 2]
                        tok = slice(t * TT, (t + 1) * TT)
                        for c in range(4):
                            xnt, xnB = xn[c]
                            pt, pB = psnext()
                            ptb = pt[:].bitcast(BF16)
                            for k in range(8):
                                S.op("pe", lambda e, k=k, ptb=ptb, xnt=xnt: e.transpose(
                                    out=ptb[:, k * 128:(k + 1) * 128], in_=xnt[:, k * 128:(k + 1) * 128], identity=ident[:]),
                                    reads=[xnB, identB], writes=[pB])
                            evac_copy(hTt[:, :, c * 128:(c + 1) * 128], ptb[:, 0:1024].rearrange("p (k n) -> p k n", k=8),
                                      [pB], [hTB])
                        S.dma("sp", "hT%d" % (t % 2), hT_d[:, :, tok].rearrange("k p n -> p k n"), hTt[:],
                              reads=[hTB], writes=[DB("hT", t)])

                    p1_load(0)
                    p1_norm_elem(0)
                    if NTILES > 1:
                        p1_load(1)
                    p1_norm_pe(0)
                    for t in range(NTILES):
                        hTt, hTB = hT[t % 2]
                        tok = slice(t * TT, (t + 1) * TT)
                        if t + 1 < NTILES:
                            p1_norm_elem(t + 1)
                            if t + 2 < NTILES:
                                p1_load(t + 2)

                        def proj(col, pt, pB):
                            for k in range(8):
                                S.op("pe", lambda e, k=k: e.matmul(pt[:, :], lhsT=W1[:, k, col:col + 128], rhs=hTt[:, k, :],
                                                                   start=(k == 0), stop=(k == 7)),
                                     reads=[W1R(col), hTB], writes=[pB])

                        for c in range(4):
                            pt, pB = psnext()
                            for k in range(8):
                                S.op("pe", lambda e, k=k, pt=pt, c=c: e.matmul(
                                    pt[:, :], lhsT=hTt[:, k, c * 128:(c + 1) * 128], rhs=W1[:, k, 1536:2048],
                                    start=(k == 0), stop=(k == 7)), reads=[W1R(1536), hTB], writes=[pB])
                            bs, bsB = bst[c]
                            vnt, vnB = vn[c]
                            S.op("dve", lambda e, bs=bs, pt=pt: e.bn_stats(out=bs[:, 0:6], in_=pt[:, :]), reads=[pB], writes=[bsB])
                            S.op("dve", lambda e, bs=bs: e.bn_aggr(out=bs[:, 6:8], in_=bs[:, 0:6]), reads=[bsB], writes=[bsB])
                            S.op("dve", lambda e, bs=bs: e.tensor_scalar(out=bs[:, 7:8], in0=bs[:, 7:8], scalar1=1.0,
                                                                         scalar2=EPS, op0=ALU.mult, op1=ALU.add),
                                 reads=[bsB], writes=[bsB])
                            S.op("act", lambda e, bs=bs: e.sqrt(out=bs[:, 7:8], in_=bs[:, 7:8]), reads=[bsB], writes=[bsB])
                            S.op("dve", lambda e, bs=bs: e.reciprocal(out=bs[:, 7:8], in_=bs[:, 7:8]), reads=[bsB], writes=[bsB])
                            S.op("dve", lambda e, bs=bs, pt=pt, vnt=vnt: e.tensor_scalar(
                                out=vnt[:], in0=pt[:, :], scalar1=bs[:, 6:7], scalar2=bs[:, 7:8], op0=ALU.subtract,
                                op1=ALU.mult), reads=[pB, bsB], writes=[vnB])
                        for g in range(4):
                            pt, pB = psnext()
                            proj(1024 + g * 128, pt, pB)
                            evac_copy(u_s[:, g, :], pt[:, :], [pB], [u_sB])
                            pt, pB = psnext()
                            proj(2048 + g * 128, pt, pB)
                            S.op("act", lambda e, g=g, pt=pt: e.activation(out=sgb_s[:, g, :], in_=pt[:, :], func=AF.Silu),
                                 reads=[pB], writes=[sgb_sB])
                        for g in range(4):
                            pt, pB = psnext()
                            for c in range(4):
                                vnt, vnB = vn[c]
                                S.op("pe", lambda e, g=g, c=c, pt=pt, vnt=vnt: e.matmul(
                                    pt[:, c * 128:(c + 1) * 128], lhsT=vnt[:, g * 128:(g + 1) * 128], rhs=wsT[:, g, :],
                                    start=True, stop=True), reads=[vnB, wsTB], writes=[pB])
                            a1, a1B = t1[g % 2]
                            a2, a2B = t2[g % 2]
                            S.op("dve", lambda e, g=g, pt=pt, a1=a1: e.scalar_tensor_tensor(
                                out=a1[:], in0=pt[:, :], scalar=lng[:, g:g + 1], in1=Rt[:, g, :],
                                op0=ALU.mult, op1=ALU.add), reads=[pB, lngB, RtB], writes=[a1B])
                            S.op("pool", lambda e, g=g, a2=a2: e.tensor_tensor(out=a2[:], in0=u_s[:, g, :], in1=sgb_s[:, g, :],
                                                                               op=ALU.mult),
                                 reads=[u_sB, sgb_sB], writes=[a2B])
                            S.op("pool", lambda e, g=g, a1=a1, a2=a2: e.tensor_tensor(out=bo_o[:, g, :], in0=a1[:], in1=a2[:],
                                                                                      op=ALU.mult),
                                 reads=[a1B, a2B], writes=[bo_oB])
                        S.dma("sp", "bo_o", boutT_d[:, :, tok].rearrange("k p n -> p k n"), bo_o[:],
                              reads=[bo_oB], writes=[DB("bout", t)])
                        if t == 0:
                            w1_load((0, 1, 5, 6, 7, 8, 9))
                        if t + 1 < NTILES:
                            p1_norm_pe(t + 1)
                        for g in range(4):
                            pt, pB = psnext()
                            proj(g * 128, pt, pB)
                            evac_copy(xa_o[:, g, :], pt[:, :], [pB], [xa_oB])
                        S.dma("sp", "xa_o", xaT_d[:, :, 8 + t * TT:8 + (t + 1) * TT].rearrange("k p n -> p k n"), xa_o[:],
                              reads=[xa_oB], writes=[DB("xa", t)])
                        for g in range(4):
                            pt, pB = psnext()
                            proj(512 + g * 128, pt, pB)
                            S.op("act", lambda e, g=g, pt=pt: e.activation(out=sga_o[:, g, :], in_=pt[:, :], func=AF.Silu),
                                 reads=[pB], writes=[sga_oB])
                        S.dma("sp", "sga_o", sgaT_d[:, :, tok].rearrange("k p n -> p k n"), sga_o[:],
                              reads=[sga_oB], writes=[DB("sga", t)])
                        for j in range(6):
                            pt, pB = psnext()
                            proj(2560 + j * 128, pt, pB)
                            evac_copy(q_o[:, j, :], pt[:, :], [pB], [q_oB])
                        S.dma("sp", "q_o", qT_d[:, :, tok].rearrange("k p n -> p k n"), q_o[:],
                              reads=[q_oB], writes=[DB("q", t)])
                        for j in range(6):
                            pt, pB = psnext()
                            proj(3328 + j * 128, pt, pB)
                            evac_copy(k_o[:, j, :], pt[:, :], [pB], [k_oB])
                        S.dma("sp", "k_o", kT_d[:, :, tok].rearrange("k p n -> p k n"), k_o[:],
                              reads=[k_oB], writes=[DB("k", t)])
                        for c in range(4):
                            for (c0, cw) in ((0, 512), (512, 256)):
                                pt, pB = psnext()
                                for k in range(8):
                                    S.op("pe", lambda e, k=k, pt=pt, c=c, c0=c0, cw=cw: e.matmul(
                                        pt[:, 0:cw], lhsT=hTt[:, k, c * 128:(c + 1) * 128],
                                        rhs=W1[:, k, 4096 + c0:4096 + c0 + cw], start=(k == 0), stop=(k == 7)),
                                        reads=[W1R(4096 + c0), hTB], writes=[pB])
                                ch0, nch = c0 // 128, cw // 128
                                for X in range(2):
                                    evac_copy(v_o[:, c, ch0:ch0 + nch, X * 128:X * 128 + 64],
                                              pt[:, 0:cw].rearrange("p (h b j) -> p h b j", b=2, j=64)[:, :, X, :], [pB], [v_oB])
                        S.dma("sp", "v_o", v_d[tok, :].rearrange("(c p) f -> p c f", p=128), v_o[:].rearrange("p c h f -> p c (h f)"),
                              reads=[v_oB], writes=[DB("v", t)])
                        for j in range(2):
                            pt, pB = psnext()
                            proj(4864 + j * 128, pt, pB)
                            S.op("act", lambda e, j=j, pt=pt: e.activation(out=sgc_o[:, j, :], in_=pt[:, :], func=AF.Silu),
                                 reads=[pB], writes=[sgc_oB])
                        S.dma("sp", "sgc_o", sgcT_d[:, :, tok].rearrange("k p n -> p k n"), sgc_o[:],
                              reads=[sgc_oB], writes=[DB("sgc", t)])
                    S.barrier()

            if "pa" in cfg.phases:
                with ExitStack() as es:
                    SEGM = 2 * U
                    FC = 2048
                    NFC = SEGM // FC
                    q_s, q_sB = sbt(es, "q_s", [128, SEGM], BF16)
                    k_s, k_sB = sbt(es, "k_s", [128, SEGM], BF16)
                    qp, qpB = sbt(es, "qp", [128, SEGM], BF16)
                    kp, kpB = sbt(es, "kp", [128, SEGM], BF16)
                    v_s, _ = sbt(es, "v_s", [128, SEGM // 128, 192], BF16)
                    v_hB = [Buf("v_h0"), Buf("v_h1")]
                    accA, _ = sbt(es, "accA", [128, SEGM], F32)
                    accB, _ = sbt(es, "accB", [128, SEGM], F32)
                    accAB = [Buf("accA%d" % i) for i in range(NFC)]
                    accBB = [Buf("accB%d" % i) for i in range(NFC)]
                    Etl = [sbt(es, "Et", [128, 2, 3, 256], F32) for _ in range(2)]
                    swA, swAB = sbt(es, "swA", [128, 128], F32)
                    swB, swBB = sbt(es, "swB", [128, 128], F32)
                    er = [sbt(es, "er", [128, 256], F32) for _ in range(4)]
                    pr = [sbt(es, "pr", [128, 256], BF16) for _ in range(6)]
                    rec = [sbt(es, "rec", [128, 512], F32) for _ in range(2)]
                    cq = [sbt(es, "cq", [128, 512], F32) for _ in range(2)]
                    sgc_c = [sbt(es, "sgc_c", [128, FC], BF16) for _ in range(2)]
                    co_c = [sbt(es, "co_c", [128, FC], BF16) for _ in range(2)]
                    S.dma("sp", "swA", swA[:], swA_d, writes=[swAB])
                    S.dma("sp", "swB", swB[:], swB_d, writes=[swBB])
                    ring[0] = 0
                    nd_rr = [0]
                    combos = []
                    for (s0, slen) in cfg.segs:
                        for hp in range(2):
                            for g in range(3):
                                combos.append((s0, slen, hp, g))

                    def pa_load(ci):
                        s0, slen, hp, g = combos[ci]
                        d = DILS[g]
                        ch = 2 * g + hp
                        t0, t1_ = s0 // TT, (s0 + slen) // TT
                        Et, EtB = Etl[ci % 2]
                        S.dma("sp", "q_s", q_s[:, 0:slen], qT_d[ch, :, s0:s0 + slen], reads=DBr("q", t0, t1_), writes=[q_sB])
                        S.dma("sp", "k_s", k_s[:, 0:slen], kT_d[ch, :, s0:s0 + slen], reads=DBr("k", t0, t1_), writes=[k_sB])
                        S.dma("sp", "Et%d" % (ci % 2), Et[:], E_d[4 * g + 2 * hp:4 * g + 2 * hp + 2].rearrange("h v p c -> p h v c"),
                              writes=[EtB])

                    def pa_load_v(ci, half):
                        s0, slen, hp, g = combos[ci]
                        d = DILS[g]
                        ch = 2 * g + hp
                        t0, t1_ = s0 // TT, (s0 + slen) // TT
                        ntr = slen // d // 128
                        vsrc = v_d[s0:s0 + slen, ch * 192:(ch + 1) * 192].rearrange("(jj p r) f -> r p jj f", p=128, r=d)
                        if d == 1:
                            j0, j1 = half * (ntr // 2), (half + 1) * (ntr // 2)
                            S.dma("sp", "v_h%d" % half, v_s[:, j0:j1, :], vsrc[0][:, j0:j1, :], reads=DBr("v", t0, t1_), writes=[v_hB[half]])
                        else:
                            for r in range(half * (d // 2), (half + 1) * (d // 2)):
                                S.dma("sp", "v_h%d" % half, v_s[:, r * ntr:(r + 1) * ntr, :], vsrc[r], reads=DBr("v", t0, t1_), writes=[v_hB[half]])

                    def pa_permute(ci):
                        s0, slen, hp, g = combos[ci]
                        d = DILS[g]
                        Lr = slen // d
                        nsp = 4
                        for (src, srcB, dst, dstB) in ((q_s, q_sB, qp, qpB), (k_s, k_sB, kp, kpB)):
                            for hh in range(nsp):
                                ls = slice(hh * (Lr // nsp), (hh + 1) * (Lr // nsp))
                                if d == 1:
                                    S.op("act", lambda e, src=src, dst=dst, ls=ls: e.copy(out=dst[:, ls], in_=src[:, ls]),
                                         reads=[srcB], writes=[dstB])
                                else:
                                    S.op("act", lambda e, src=src, dst=dst, ls=ls: e.copy(
                                        out=dst[:, 0:slen].rearrange("p (r l) -> p r l", r=d)[:, :, ls],
                                        in_=src[:, 0:slen].rearrange("p (l r) -> p r l", r=d)[:, :, ls]),
                                        reads=[srcB], writes=[dstB])

                    pa_load(0)
                    pa_load_v(0, 0)
                    pa_load_v(0, 1)
                    pa_permute(0)
                    for ci, (s0, slen, hp, g) in enumerate(combos):
                        if ci + 1 < len(combos):
                            pa_load(ci + 1)
                        d = DILS[g]
                        Lr = slen // d
                        nblk = Lr // 64
                        ntr = nblk // 2
                        mid = nblk // 2 if slen == 2 * U else -1
                        nfc = slen // FC
                        Et, EtB = Etl[ci % 2]
                        tiles = []
                        for r in range(d):
                            for jj in range(ntr):
                                for X in range(2):
                                    tiles.append((r, jj, X))
                        n = len(tiles)
                        st = {}
                        banks = {}

                        def get_bank(r, b):
                            key = (r, b)
                            if key not in banks:
                                i = nd_rr[0] % 2
                                nd_rr[0] += 1
                                banks[key] = (4 + 2 * i, 5 + 2 * i, set())
                            return banks[key]

                        def geom(r, jj):
                            qlo = max(0, 2 * jj - 1)
                            qhi = min(nblk, 2 * jj + 3)
                            c0 = 64 * (qlo - (2 * jj - 1))
                            if 2 * jj == mid:
                                var = 1
                            elif 2 * jj + 2 == mid:
                                var = 2
                            else:
                                var = 0
                            return qlo, qhi, c0, var

                        LAG = 3
                        for i in range(n + LAG):
                            if i < n:
                                r, jj, X = tiles[i]
                                qlo, qhi, c0, var = geom(r, jj)
                                ncol = 64 * (qhi - qlo)
                                pt, pB = psnext(0, 4)
                                K0 = r * Lr + 128 * jj
                                Q0 = r * Lr + 64 * qlo
                                S.op("pe", lambda e, pt=pt, X=X, K0=K0, Q0=Q0, ncol=ncol: e.matmul(
                                    pt[:, 0:ncol], lhsT=kp[X * 64:(X + 1) * 64, K0:K0 + 128],
                                    rhs=qp[X * 64:(X + 1) * 64, Q0:Q0 + ncol], start=True, stop=True),
                                    reads=[kpB, qpB], writes=[pB])
                                et, etB = er[i % 4]
                                ptl, ptlB = pr[i % 6]
                                S.op("act", lambda e, pt=pt, et=et, ncol=ncol: e.activation(
                                    out=et[:, 0:ncol], in_=pt[:, 0:ncol], func=AF.Exp, scale=0.125),
                                    reads=[pB], writes=[etB])
                                meng = "dve" if i % 3 == 0 else "pool"
                                S.op(meng, lambda e, et=et, ptl=ptl, X=X, var=var, c0=c0, ncol=ncol: e.tensor_tensor(
                                    out=ptl[:, 0:ncol], in0=et[:, 0:ncol], in1=Et[:, X, var, c0:c0 + ncol], op=ALU.mult),
                                    reads=[etB, EtB], writes=[ptlB])
                            j = i - LAG
                            if 0 <= j < n:
                                r, jj, X = tiles[j]
                                qlo, qhi, c0, var = geom(r, jj)
                                ptl, ptlB = pr[j % 6]
                                blk = qlo
                                while blk < qhi:
                                    b = blk // 8
                                    bend = min(qhi, (b + 1) * 8)
                                    bkA, bkB, started = get_bank(r, b)
                                    bk = bkA if X == 0 else bkB
                                    first = X not in started
                                    started.add(X)
                                    oc = 64 * (blk - 8 * b)
                                    w = 64 * (bend - blk)
                                    pc = 64 * (blk - qlo)
                                    S.op("pe", lambda e, bk=bk, X=X, oc=oc, w=w, pc=pc, ptl=ptl, r=r, jj=jj, first=first: e.matmul(
                                        ps[bk][:, oc:oc + w], lhsT=v_s[:, r * ntr + jj, X * 64:X * 64 + 128],
                                        rhs=ptl[:, pc:pc + w], start=first, stop=False, skip_group_check=True),
                                        reads=[v_hB[0 if (r * ntr + jj) < (d * ntr) // 2 else 1], ptlB], writes=[psb[bk]])
                                    blk = bend
                                if X == 1:
                                    done = [(rr, b) for (rr, b) in list(banks.keys()) if rr == r and jj == min(ntr - 1, 4 * b + 4)]
                                    for (rr, b) in done:
                                        bkA, bkB, _ = banks.pop((rr, b))
                                        nb = min(8, nblk - 8 * b) * 64
                                        l0 = 512 * b
                                        for (acc, accBufs, bk, eng0) in ((accA, accAB, bkA, "act"), (accB, accBB, bkB, "dve")):
                                            if d == 1:
                                                ov = acc[:, l0:l0 + nb]
                                                bl = [accBufs[l0 // FC]]
                                            else:
                                                ov = acc[:, 0:slen].rearrange("p (l r) -> p r l", r=d)[:, rr, l0:l0 + nb]
                                                bl = accBufs[0:nfc]
                                            if g == 0:
                                                if eng0 == "act":
                                                    S.op("act", lambda e, ov=ov, bk=bk, nb=nb: e.copy(out=ov, in_=ps[bk][:, 0:nb]),
                                                         reads=[psb[bk]], writes=bl)
                                                else:
                                                    S.op("dve", lambda e, ov=ov, bk=bk, nb=nb: e.tensor_copy(out=ov, in_=ps[bk][:, 0:nb]),
                                                         reads=[psb[bk]], writes=bl)
                                            else:
                                                S.op("dve", lambda e, ov=ov, bk=bk, nb=nb: e.tensor_tensor(
                                                    out=ov, in0=ps[bk][:, 0:nb], in1=ov, op=ALU.add),
                                                    reads=[psb[bk]] + bl, writes=bl)
                            if j == n // 2 and ci + 1 < len(combos):
                                pa_load_v(ci + 1, 0)
                        assert not banks, banks
                        if ci + 1 < len(combos):
                            pa_load_v(ci + 1, 1)
                            pa_permute(ci + 1)
                        if g == 2:
                            for fc in range(nfc):
                                sg, sgB = sgc_c[fc % 2]
                                co, coB = co_c[fc % 2]
                                tf0 = (s0 + fc * FC) // TT
                                S.dma("sp", "sgc_c%d" % (fc % 2), sg[:], sgcT_d[hp, :, s0 + fc * FC:s0 + (fc + 1) * FC],
                                      reads=DBr("sgc", tf0, tf0 + FC // TT), writes=[sgB])
                                for pc_ in range(FC // 512):
                                    cs = slice(fc * FC + pc_ * 512, fc * FC + (pc_ + 1) * 512)
                                    ls_ = slice(pc_ * 512, (pc_ + 1) * 512)
                                    pt, pB = psnext(0, 4)
                                    S.op("pe", lambda e, pt=pt, cs=cs: e.matmul(pt[:, :], lhsT=swA[:, :], rhs=accA[:, cs], start=True, stop=False),
                                         reads=[swAB, accAB[fc]], writes=[pB])
                                    S.op("pe", lambda e, pt=pt, cs=cs: e.matmul(pt[:, :], lhsT=swB[:, :], rhs=accB[:, cs], start=False, stop=True),
                                         reads=[swBB, accBB[fc]], writes=[pB])
                                    rc, rcB = rec[pc_ % 2]
                                    cqt, cqB = cq[pc_ % 2]
                                    S.op("dve", lambda e, rc=rc, pt=pt: e.reciprocal(out=rc[:], in_=pt[:, :]), reads=[pB], writes=[rcB])
                                    S.op("pool", lambda e, rc=rc, cqt=cqt, cs=cs: e.tensor_tensor(
                                        out=cqt[0:64, :], in0=accA[0:64, cs], in1=rc[0:64, :], op=ALU.mult),
                                        reads=[accAB[fc], rcB], writes=[cqB])
                                    S.op("pool", lambda e, rc=rc, cqt=cqt, cs=cs: e.tensor_tensor(
                                        out=cqt[64:128, :], in0=accB[64:128, cs], in1=rc[64:128, :], op=ALU.mult),
                                        reads=[accBB[fc], rcB], writes=[cqB])
                                    S.op("pool", lambda e, cqt=cqt, sg=sg, co=co, ls_=ls_: e.tensor_tensor(
                                        out=co[:, ls_], in0=cqt[:], in1=sg[:, ls_], op=ALU.mult),
                                        reads=[cqB, sgB], writes=[coB])
                                S.dma("sp", "co_c%d" % (fc % 2), coutT_d[hp, :, s0 + fc * FC:s0 + (fc + 1) * FC], co[:],
                                      reads=[coB], writes=DBr("cout", tf0, tf0 + FC // TT))
                    S.barrier()

            if "p2" in cfg.phases:
                with ExitStack() as es:
                    Wmg, WmgB = sbt(es, "Wmg", [128, 8, 3072], BF16)
                    Wa, WaB = sbt(es, "Wa", [128, 4, D], BF16)
                    Wb, WbB = sbt(es, "Wb", [128, 4, D], BF16)
                    Wc, WcB = sbt(es, "Wc", [128, 2, D], BF16)
                    Wo, WoB = sbt(es, "Wo", [128, 8, D], BF16)
                    pw, pwB = sbt(es, "pw", [128, 4, 128], BF16)
                    psc, pscB = sbt(es, "psc", [128, 4], F32)
                    flg, flgB = sbt(es, "flg", [128, 1], F32)
                    fgb, fgbB = sbt(es, "fgb", [128, D], F32)
                    WmgBs = [Buf("Wmg%d" % i) for i in range(6)]
                    S.dma("pool", "pw", pw[:], pool_w[l].rearrange("g c d -> c g d"), writes=[pwB])
                    for j in (0, 2, 4):
                        S.dma("pool", "Wmg%d" % j, Wmg[:, :, j * 512:(j + 1) * 512],
                              w_in[l, :, 5120 + j * 512:5120 + (j + 1) * 512].rearrange("(k p) n -> p k n", p=128), writes=[WmgBs[j]])
                    S.dma("pool", "Wa", Wa[:], w_bra[l].rearrange("(k p) n -> p k n", p=128), writes=[WaB])
                    S.dma("pool", "Wb", Wb[:], w_brb[l].rearrange("(k p) n -> p k n", p=128), writes=[WbB])
                    S.dma("pool", "Wc", Wc[:], w_brc[l].rearrange("(k p) n -> p k n", p=128), writes=[WcB])
                    def w2_late():
                        for j in (1, 3, 5):
                            S.dma("pool", "Wmg%d" % j, Wmg[:, :, j * 512:(j + 1) * 512],
                                  w_in[l, :, 5120 + j * 512:5120 + (j + 1) * 512].rearrange("(k p) n -> p k n", p=128), writes=[WmgBs[j]])
                        for j in range(2):
                            S.dma("pool", "Wo", Wo[:, :, j * 512:(j + 1) * 512],
                                  w_out[l, :, j * 512:(j + 1) * 512].rearrange("(k p) n -> p k n", p=128), writes=[WoB])
                    S.dma("sp", "psc", psc[:], pscale_t[l], writes=[pscB])
                    S.dma("sp", "flg", flg[:], flag_d, writes=[flgB])
                    if last:
                        S.dma("sp", "fgb", fgb[:], final_g[0:1, :].partition_broadcast(128), writes=[fgbB])

                    hT2 = [sbt(es, "hT2", [128, 8, TT], BF16) for _ in range(2)]
                    xs2 = [sbt(es, "xs2", [128, D], F32) for _ in range(4)]
                    xa2 = [sbt(es, "xa2", [128, 4, TT + 16], BF16) for _ in range(2)]
                    sga2 = [sbt(es, "sga2", [128, 4, TT], BF16) for _ in range(2)]
                    bo2 = [sbt(es, "bo2", [128, 4, TT], BF16) for _ in range(2)]
                    co2 = [sbt(es, "co2", [128, 2, TT], BF16) for _ in range(2)]
                    pinv, pinvB = sbt(es, "pinv", [128, 4, TT], F32)
                    sA, sAB = sbt(es, "sA", [128, TT + 16], F32)
                    sB_, sBB = sbt(es, "sB", [128, TT + 16], F32)
                    sC, sCB = sbt(es, "sC", [128, TT + 16], F32)
                    mx = [sbt(es, "mx", [128, TT], BF16) for _ in range(4)]
                    aol = [sbt(es, "ao", [128, 4, TT], BF16) for _ in range(2)]
                    sg = [sbt(es, "sg", [128, TT], F32) for _ in range(4)]
                    tm = [sbt(es, "tm", [128, TT], F32) for _ in range(4)]
                    mg, mgB = sbt(es, "mg", [128, 8, TT], BF16)
                    junk2, junk2B = sbt(es, "junk2", [128, D], BF16)
                    ss2 = [sbt(es, "ss2", [128, 2], F32) for _ in range(4)]
                    ring[0] = 0

                    bnd = {0: (0, "L", "hard"), TPU - 1: (1, "R", "flag"), TPU: (2, "L", "flag"), 2 * TPU - 1: (3, "R", "hard"),
                           2 * TPU: (4, "L", "hard"), 3 * TPU - 1: (5, "R", "hard")}

                    def p2_load(t):
                        s = t % 2
                        tok = slice(t * TT, (t + 1) * TT)
                        S.dma("sp", "hT2_%d" % s, hT2[s][0][:], hT_d[:, :, tok].rearrange("k p n -> p k n"),
                              reads=[DB("hT", t)], writes=[hT2[s][1]])
                        S.dma("sp", "xa2_%d" % s, xa2[s][0][:], xaT_d[:, :, t * TT:t * TT + TT + 16].rearrange("k p n -> p k n"),
                              reads=[DB("xa", tt) for tt in (t - 1, t, t + 1) if 0 <= tt < NTILES], writes=[xa2[s][1]])
                        S.dma("sp", "sga2_%d" % s, sga2[s][0][:], sgaT_d[:, :, tok].rearrange("k p n -> p k n"),
                              reads=[DB("sga", t)], writes=[sga2[s][1]])
                        S.dma("sp", "bo2_%d" % s, bo2[s][0][:], boutT_d[:, :, tok].rearrange("k p n -> p k n"),
                              reads=[DB("bout", t)], writes=[bo2[s][1]])
                        S.dma("sp", "co2_%d" % s, co2[s][0][:], coutT_d[:, :, tok].rearrange("k p n -> p k n"),
                              reads=[DB("cout", t)], writes=[co2[s][1]])

                    def p2_pool_elem(t, groups=(0, 1, 2, 3)):
                        s = t % 2
                        xat, xaB = xa2[s]
                        binfo = bnd.get(t)
                        if binfo is not None and 0 in groups:
                            bi, side, kind = binfo
                            S.dma("sp", "pinv", pinv[:].rearrange("p g n -> p (g n)"),
                                  pinv_d[bi:bi + 1].rearrange("o g n -> o (g n)").partition_broadcast(128), writes=[pinvB])
                            hs = slice(0, 8) if side == "L" else slice(TT + 8, TT + 16)
                            if kind == "hard":
                                S.op("pool", lambda e, xat=xat, hs=hs: e.memset(xat[:, :, hs], 0.0), reads=[xaB], writes=[xaB])
                            else:
                                S.op("pool", lambda e, xat=xat, hs=hs: e.tensor_scalar(
                                    out=xat[:, :, hs], in0=xat[:, :, hs], scalar1=flg[:, 0:1], scalar2=None, op0=ALU.mult),
                                    reads=[xaB, flgB], writes=[xaB])
                        NW = TT + 16
                        for g in groups:
                            w = POOL_WINDOWS[g]
                            xg = xat[:, g, :]
                            S.op("pool", lambda e, xg=xg: e.tensor_tensor(out=sA[:, 1:NW], in0=xg[:, 0:NW - 1], in1=xg[:, 1:NW], op=ALU.add),
                                 reads=[xaB], writes=[sAB])
                            cur, curB = sA, sAB
                            if w >= 4:
                                S.op("pool", lambda e: e.tensor_tensor(out=sB_[:, 2:NW - 1], in0=sA[:, 1:NW - 2], in1=sA[:, 3:NW], op=ALU.add),
                                     reads=[sAB], writes=[sBB])
                                cur, curB = sB_, sBB
                            if w >= 8:
                                S.op("pool", lambda e: e.tensor_tensor(out=sC[:, 4:NW - 3], in0=sB_[:, 2:NW - 5], in1=sB_[:, 6:NW - 1], op=ALU.add),
                                     reads=[sBB], writes=[sCB])
                                cur, curB = sC, sCB
                            if w >= 16:
                                S.op("pool", lambda e: e.tensor_tensor(out=sA[:, 8:NW - 7], in0=sC[:, 4:NW - 11], in1=sC[:, 12:NW - 3], op=ALU.add),
                                     reads=[sCB], writes=[sAB])
                                cur, curB = sA, sAB
                            mxt, mxB = mx[g]
                            if binfo is None:
                                S.op("dve", lambda e, cur=cur, xg=xg, mxt=mxt, w=w: e.scalar_tensor_tensor(
                                    out=mxt[:], in0=cur[:, 8:8 + TT], scalar=1.0 / w, in1=xg[:, 8:8 + TT], op0=ALU.mult, op1=ALU.subtract),
                                    reads=[curB, xaB], writes=[mxB])
                            else:
                                S.op("pool", lambda e, cur=cur, g=g: e.tensor_tensor(out=cur[:, 8:8 + TT], in0=cur[:, 8:8 + TT], in1=pinv[:, g, :], op=ALU.mult),
                                     reads=[curB, pinvB], writes=[curB])
                                S.op("pool", lambda e, cur=cur, xg=xg, mxt=mxt: e.tensor_tensor(out=mxt[:], in0=cur[:, 8:8 + TT], in1=xg[:, 8:8 + TT], op=ALU.subtract),
                                     reads=[curB, xaB], writes=[mxB])

                    def p2_pool_mm(t):
                        s = t % 2
                        sgat, sgaB = sga2[s]
                        aot, aotB = aol[s]
                        for g in range(4):
                            mxt, mxB = mx[g]
                            pt, pB = psnext()
                            S.op("pe", lambda e, g=g, pt=pt, mxt=mxt: e.matmul(pt[:, :], lhsT=pw[:, g, :], rhs=mxt[:], start=True, stop=True),
                                 reads=[pwB, mxB], writes=[pB])
                            S.op("dve", lambda e, g=g, pt=pt, sgat=sgat, aot=aot: e.scalar_tensor_tensor(
                                out=aot[:, g, :], in0=pt[:, :], scalar=psc[:, g:g + 1], in1=sgat[:, g, :], op0=ALU.mult, op1=ALU.mult),
                                reads=[pB, pscB, sgaB], writes=[aotB])

                    pre_g = [None]

                    def emit_gates(tt, m):
                        hTg, hTgB = hT2[tt % 2]
                        out = []
                        for bi_ in range(3):
                            pg, pgB = psnext()
                            col = bi_ * D + m * 128
                            for k in range(8):
                                S.op("pe", lambda e, k=k, pg=pg, col=col: e.matmul(pg[:, :], lhsT=Wmg[:, k, col:col + 128], rhs=hTg[:, k, :],
                                                                                  start=(k == 0), stop=(k == 7)),
                                     reads=[WmgBs[col // 512], hTgB], writes=[pgB])
                            idx = (m * 3 + bi_) % 4
                            sgt, sgB = sg[idx]
                            S.op("act", lambda e, pg=pg, sgt=sgt: e.activation(out=sgt[:], in_=pg[:, :], func=AF.Sigmoid),
                                 reads=[pgB], writes=[sgB])
                            out.append((sgt, sgB, idx))
                        return out

                    p2_load(0)
                    p2_pool_elem(0)
                    p2_pool_mm(0)
                    for t in range(NTILES):
                        s = t % 2
                        hTt, hTB = hT2[s]
                        bot, boB = bo2[s]
                        cot, coB = co2[s]
                        ao, aoB = aol[s]
                        for c in range(4):
                            xt, xB = xs2[c]
                            r0 = t * TT + c * 128
                            S.dma("sp", "xs2_%d" % c, xt[:], x_src[r0:r0 + 128, :], reads=[DB("x", t)], writes=[xB])
                        if t + 1 < NTILES:
                            p2_load(t + 1)
                        brs = ((Wa, WaB, 4, ao, aoB), (Wb, WbB, 4, bot, boB), (Wc, WcB, 2, cot, coB))
                        for m in range(8):
                            if m == 0 and pre_g[0] is not None:
                                G = pre_g[0]
                                pre_g[0] = None
                            else:
                                G = emit_gates(t, m)
                            prods = []
                            for bi_, (Wbr, WbrB, nk, src, srcB) in enumerate(brs):
                                pb_, pbB = psnext()
                                for k in range(nk):
                                    S.op("pe", lambda e, k=k, pb_=pb_, Wbr=Wbr, src=src, nk=nk: e.matmul(
                                        pb_[:, :], lhsT=Wbr[:, k, m * 128:(m + 1) * 128], rhs=src[:, k, :], start=(k == 0), stop=(k == nk - 1)),
                                        reads=[WbrB, srcB], writes=[pbB])
                                sgt, sgB, idx = G[bi_]
                                tmt, tmB = tm[idx]
                                S.op("dve", lambda e, pb_=pb_, sgt=sgt, tmt=tmt: e.tensor_tensor(out=tmt[:], in0=pb_[:, :], in1=sgt[:], op=ALU.mult),
                                     reads=[pbB, sgB], writes=[tmB])
                                prods.append((tmt, tmB))
                            (ta, taB), (tb_, tbB_), (tc, tcB) = prods
                            S.op("pool", lambda e, ta=ta, tb_=tb_: e.tensor_tensor(out=ta[:], in0=ta[:], in1=tb_[:], op=ALU.add),
                                 reads=[taB, tbB_], writes=[taB])
                            S.op("pool", lambda e, ta=ta, tc=tc, m=m: e.tensor_tensor(out=mg[:, m, :], in0=ta[:], in1=tc[:], op=ALU.add),
                                 reads=[taB, tcB], writes=[mgB])
                            if t + 1 < NTILES and m < 4:
                                p2_pool_elem(t + 1, groups=(m,))
                            if t == 0 and m == 1:
                                w2_late()
                        if t + 1 < NTILES:
                            pre_g[0] = emit_gates(t + 1, 0)
                            p2_pool_mm(t + 1)
                        for c in range(4):
                            xt, xB = xs2[c]
                            for hf in range(2):
                                pt, pB = psnext()
                                for k in range(8):
                                    S.op("pe", lambda e, k=k, pt=pt, c=c, hf=hf: e.matmul(
                                        pt[:, :], lhsT=mg[:, k, c * 128:(c + 1) * 128], rhs=Wo[:, k, hf * 512:(hf + 1) * 512],
                                        start=(k == 0), stop=(k == 7)), reads=[mgB, WoB], writes=[pB])
                                S.op("dve", lambda e, pt=pt, xt=xt, hf=hf: e.tensor_tensor(
                                    out=xt[:, hf * 512:(hf + 1) * 512], in0=pt[:, :], in1=xt[:, hf * 512:(hf + 1) * 512], op=ALU.add),
                                    reads=[pB, xB], writes=[xB])
                            r0 = t * TT + c * 128
                            if not last:
                                S.dma("sp", "xs2_%d" % c, xres_d[r0:r0 + 128, :], xt[:], reads=[xB], writes=[DB("x", t)])
                            else:
                                sq, sqB = ss2[c]
                                S.op("act", lambda e, xt=xt, sq=sq: e.activation(out=junk2[:], in_=xt[:], func=AF.Square, accum_out=sq[:, 0:1]),
                                     reads=[xB], writes=[junk2B, sqB])
                                S.op("dve", lambda e, sq=sq: e.tensor_scalar(out=sq[:, 1:2], in0=sq[:, 0:1], scalar1=1.0 / D, scalar2=EPS,
                                                                             op0=ALU.mult, op1=ALU.add), reads=[sqB], writes=[sqB])
                                S.op("act", lambda e, sq=sq: e.sqrt(out=sq[:, 1:2], in_=sq[:, 1:2]), reads=[sqB], writes=[sqB])
                                S.op("dve", lambda e, sq=sq: e.reciprocal(out=sq[:, 1:2], in_=sq[:, 1:2]), reads=[sqB], writes=[sqB])
                                S.op("dve", lambda e, xt=xt, sq=sq: e.scalar_tensor_tensor(
                                    out=xt[:], in0=xt[:], scalar=sq[:, 1:2], in1=fgb[:], op0=ALU.mult, op1=ALU.mult),
                                    reads=[xB, sqB, fgbB], writes=[xB])
                                S.dma("sp", "xs2_%d" % c, y_d[r0:r0 + 128, :], xt[:], reads=[xB], writes=[DB("y", t)])
                    S.barrier()
        S.barrier()
    return nc, S


def _bf(a):
    return a


def host_consts(unit):
    ident = np.eye(128, dtype=np.float32)
    anti = np.ascontiguousarray(ident[::-1])
    oh = np.zeros((32, 3, 384), np.float32)
    m = np.arange(384)
    rel = 191 - m
    for g, d in enumerate(DILS):
        b = _t5_bucket(rel * d)
        oh[b, g, m] = 1.0
    valid = (np.abs(rel) <= 64).astype(np.float32)
    valid = np.ascontiguousarray(np.broadcast_to(valid[None, :], (4, 384)))
    return ident, anti, oh, valid


def swap_consts():
    swA = np.zeros((128, 128), np.float32)
    swB = np.zeros((128, 128), np.float32)
    for m in range(64):
        swA[m + 64, m] = 1.0
        swB[m, m + 64] = 1.0
    return {"swA": swA, "swB": swB}


def core_consts(unit, linked):
    flag = np.full((128, 1), 1.0 if linked else 0.0, np.float32)
    mL = np.ones((128, 256), np.float32)
    mR = np.ones((128, 256), np.float32)
    if not linked:
        mL[0:64, 0:64] = 0.0
        mR[64:128, 192:256] = 0.0
    seqs = [(0, 2 * unit)] if linked else [(0, unit), (unit, unit)]
    seqs.append((2 * unit, unit))
    nt = 3 * unit
    inv = np.zeros((4, nt), np.float32)
    for (s0, sl) in seqs:
        t = np.arange(sl)
        for gi, w in enumerate(POOL_WINDOWS):
            lo = np.clip(t - w // 2, 0, sl - 1)
            hi = np.clip(t + w // 2 - 1, 0, sl - 1)
            inv[gi, s0:s0 + sl] = 1.0 / (hi - lo + 1)
    tpu = unit // TT
    btiles = [0, tpu - 1, tpu, 2 * tpu - 1, 2 * tpu, 3 * tpu - 1]
    pinv = np.stack([inv[:, bt * TT:(bt + 1) * TT] for bt in btiles]).astype(np.float32)
    return flag, mL, mR, pinv


def make_shared(inputs, depth):
    f = lambda a: np.ascontiguousarray(np.asarray(a, dtype=np.float32))
    L = depth
    sh = {
        "w_in": f(inputs["w_in"])[:L], "w_br_a": f(inputs["w_br_a"])[:L], "w_br_b": f(inputs["w_br_b"])[:L],
        "w_br_c": f(inputs["w_br_c"])[:L], "w_out": f(inputs["w_out"])[:L], "pool_w": f(inputs["pool_w"])[:L],
        "sgu_wT": np.ascontiguousarray(f(inputs["sgu_w"])[:L].transpose(0, 1, 3, 2)),
        "norm_g": f(inputs["norm_g"])[:L], "final_g": f(inputs["final_g"]).reshape(1, D),
        "pscale_t": np.ascontiguousarray(f(inputs["pool_scale"])[:L].reshape(L, 4, 128).transpose(0, 2, 1)),
        "lng_t": np.ascontiguousarray(f(inputs["sgu_ln_g"])[:L].reshape(L, 4, 128).transpose(0, 2, 1)),
        "sgu_ln_b": f(inputs["sgu_ln_b"])[:L], "sgu_b": f(inputs["sgu_b"])[:L].reshape(L, 512),
        "rel_bias": f(inputs["rel_bias"]),
    }
    return sh


_NC_CACHE = {}


def kernel(**inputs):
    unit, depth = 4096, 4
    xp = np.asarray(inputs["x_prompt"], dtype=np.float32)
    xs = np.asarray(inputs["x_sample"], dtype=np.float32)
    ident, anti, oh, valid = host_consts(unit)
    sh = make_shared(inputs, depth)
    sh.update({"ident": ident, "anti": anti, "oh": oh, "valid": valid})
    sh.update(swap_consts())
    in_maps = []
    for c in range(8):
        x = np.zeros((3 * unit, D), np.float32)
        if c < 2:
            x[0:2 * unit] = xp[c]
        else:
            x[0:unit] = xs[2 * (c - 2)]
            x[unit:2 * unit] = xs[2 * (c - 2) + 1]
        if c < 4:
            x[2 * unit:] = xs[12 + c]
        flag, mL, mR, pinv = core_consts(unit, c < 2)
        m = dict(sh)
        m.update({"x": x, "flag": flag, "mL": mL, "mR": mR, "pinv": pinv})
        in_maps.append(m)
    if "nc" not in _NC_CACHE:
        _NC_CACHE["nc"] = build(Cfg(unit=unit, depth=depth))[0]
    nc = _NC_CACHE["nc"]
    res = run_bass_kernel_spmd(nc, in_maps, core_ids=list(range(8)))
    yp = np.zeros_like(xp)
    ys = np.zeros_like(xs)
    for c in range(8):
        y = np.asarray(res.results[c]["y"], dtype=np.float32)
        if c < 2:
            yp[c] = y[0:2 * unit]
        else:
            ys[2 * (c - 2)] = y[0:unit]
            ys[2 * (c - 2) + 1] = y[unit:2 * unit]
        if c < 4:
            ys[12 + c] = y[2 * unit:]
    return (yp, ys)
```

```python
import numpy as np
import ml_dtypes
from contextlib import ExitStack
import concourse.bass as bass
import concourse.mybir as mybir
from concourse.bass_utils import run_bass_kernel_spmd

F32 = mybir.dt.float32
BF16 = mybir.dt.bfloat16
AF = mybir.ActivationFunctionType
ALU = mybir.AluOpType

D = 1024
NIN = 8192
EPS = 1e-6
POOL_WINDOWS = (2, 4, 8, 16)
DILS = (1, 4, 16)
TT = 512


def _t5_bucket(rel):
    N_BUCKETS, T5_MAX_DIST = 32, 1024
    half = N_BUCKETS // 2
    n = -rel
    ret = (n < 0).astype(np.int32) * half
    n = np.abs(n)
    max_exact = half // 2
    large = max_exact + (np.log(np.maximum(n, 1) / max_exact) / np.log(T5_MAX_DIST / max_exact)
                         * (half - max_exact)).astype(np.int32)
    large = np.minimum(large, half - 1)
    return (ret + np.where(n < max_exact, n, large)).astype(np.int32)


class Buf:
    __slots__ = ("w", "rs", "name")

    def __init__(self, name=""):
        self.w = None
        self.rs = {}
        self.name = name


class Sync:
    def __init__(self, nc):
        self.nc = nc
        self.eng = {"pe": nc.tensor, "act": nc.scalar, "dve": nc.vector, "pool": nc.gpsimd, "sp": nc.sync}
        self.semh = {}
        self.cnt = {}
        self.cur = {}
        self.epoch = 0
        self.waited = {e: {} for e in self.eng}
        self.nwaits = 0
        self.ninst = 0
        self._new_engine_sems()

    def _new_engine_sems(self):
        for e in ("pe", "act", "dve", "pool"):
            s = "E_%s_%d" % (e, self.epoch)
            self.semh[s] = self.nc.alloc_semaphore(name="sem_%s_%d" % (e, self.epoch))
            self.cnt[s] = 0
            self.cur[e] = s

    def new_epoch(self):
        self.barrier()
        self.epoch += 1
        self._new_engine_sems()

    def _deps(self, reads, writes):
        deps = {}
        for b in reads:
            if b.w is not None:
                s, v = b.w
                if deps.get(s, 0) < v:
                    deps[s] = v
        for b in writes:
            if b.w is not None:
                s, v = b.w
                if deps.get(s, 0) < v:
                    deps[s] = v
            for s, v in b.rs.items():
                if deps.get(s, 0) < v:
                    deps[s] = v
        return deps

    def _wait(self, e, deps):
        eng = self.eng[e]
        wd = self.waited[e]
        own_pe = self.cur.get("pe") if e == "pe" else None
        for s, v in deps.items():
            if s == own_pe:
                continue
            if wd.get(s, 0) < v:
                eng.wait_ge(self.semh[s], v)
                wd[s] = v
                self.nwaits += 1

    def _post(self, ev, reads, writes):
        s, v = ev
        for b in writes:
            b.w = ev
            b.rs = {}
        for b in reads:
            if b.rs.get(s, 0) < v:
                b.rs[s] = v

    def op(self, e, fn, reads=(), writes=()):
        self._wait(e, self._deps(reads, writes))
        ins = fn(self.eng[e])
        s = self.cur[e]
        self.cnt[s] += 1
        ins.then_inc(self.semh[s], 1)
        self.ninst += 1
        self._post((s, self.cnt[s]), reads, writes)

    def dma(self, q, key, out, in_, reads=(), writes=(), **kw):
        s = "D_" + key
        if s not in self.semh:
            self.semh[s] = self.nc.alloc_semaphore(name="dsem_" + key)
            self.cnt[s] = 0
        self._wait(q, self._deps(reads, writes))
        ins = self.eng[q].dma_start(out=out, in_=in_, **kw)
        self.cnt[s] += 16
        ins.then_inc(self.semh[s], 16)
        self.ninst += 1
        self._post((s, self.cnt[s]), reads, writes)

    def barrier(self):
        allev = {s: c for s, c in self.cnt.items() if c > 0}
        for e in self.eng:
            self._wait(e, allev)


class Cfg:
    def __init__(self, unit=4096, depth=4, debug=False, phases=("p1", "pa", "p2")):
        self.unit = unit
        self.depth = depth
        self.debug = debug
        self.phases = phases
        self.nt = 3 * unit
        self.ntiles = self.nt // TT
        self.segs = ((0, 2 * unit), (2 * unit, unit))


def build(cfg):
    nc = bass.Bass("TRN2", target_bir_lowering=False)
    S = Sync(nc)
    U, NT, L = cfg.unit, cfg.nt, cfg.depth
    NTILES = cfg.ntiles
    TPU = U // TT

    def din(name, shape, dt=F32):
        return nc.dram_tensor(name, list(shape), dt, kind="ExternalInput").ap()

    def dscr(name, shape, dt):
        kind = "ExternalOutput" if cfg.debug else "Internal"
        return nc.dram_tensor(name, list(shape), dt, kind=kind).ap()

    x_in = din("x", [NT, D])
    w_in = din("w_in", [L, D, NIN])
    w_bra = din("w_br_a", [L, 512, D])
    w_brb = din("w_br_b", [L, 512, D])
    w_brc = din("w_br_c", [L, 256, D])
    w_out = din("w_out", [L, D, D])
    pool_w = din("pool_w", [L, 4, 128, 128])
    sgu_wT = din("sgu_wT", [L, 4, 128, 128])
    norm_g = din("norm_g", [L, D])
    final_g = din("final_g", [1, D])
    pscale_t = din("pscale_t", [L, 128, 4])
    lng_t = din("lng_t", [L, 128, 4])
    lnb = din("sgu_ln_b", [L, 512])
    sgub = din("sgu_b", [L, 512])
    rel_bias = din("rel_bias", [32, 12])
    ident_d = din("ident", [128, 128])
    anti_d = din("anti", [128, 128])
    oh_d = din("oh", [32, 3, 384])
    valid_d = din("valid", [4, 384])
    flag_d = din("flag", [128, 1])
    mL_d = din("mL", [128, 256])
    mR_d = din("mR", [128, 256])
    swA_d = din("swA", [128, 128])
    swB_d = din("swB", [128, 128])
    pinv_d = din("pinv", [6, 4, TT])

    y_d = nc.dram_tensor("y", [NT, D], F32, kind="ExternalOutput").ap()
    xres_d = dscr("xres", [NT, D], F32)
    hT_d = dscr("hT", [8, 128, NT], BF16)
    qT_d = dscr("qT", [6, 128, NT], BF16)
    kT_d = dscr("kT", [6, 128, NT], BF16)
    v_d = dscr("vtok", [NT, 1152], BF16)
    xaT_d = dscr("xaT", [4, 128, NT + 16], BF16)
    sgaT_d = dscr("sgaT", [4, 128, NT], BF16)
    boutT_d = dscr("boutT", [4, 128, NT], BF16)
    sgcT_d = dscr("sgcT", [2, 128, NT], BF16)
    coutT_d = dscr("coutT", [2, 128, NT], BF16)
    u_d = dscr("u_scr", [12, 384], F32)
    E_d = dscr("E_scr", [12, 3, 128, 256], F32)

    dbufs = {}

    def DB(name, t):
        k = (name, t)
        if k not in dbufs:
            dbufs[k] = Buf("%s_%d" % k)
        return dbufs[k]

    def DBr(name, t0, t1):
        return [DB(name, t) for t in range(t0, t1)]

    uid = [0]

    with ExitStack() as gs:
        ps = [gs.enter_context(nc.psum_tensor("ps%d" % i, [128, 512], F32)) for i in range(8)]
        psb = [Buf("ps%d" % i) for i in range(8)]
        ring = [0]

        def psnext(lo=0, hi=8):
            i = lo + (ring[0] % (hi - lo))
            ring[0] += 1
            return ps[i], psb[i]

        def sbt(es, name, shape, dt):
            uid[0] += 1
            t = es.enter_context(nc.sbuf_tensor("%s_%d" % (name, uid[0]), list(shape), dt))
            return t, Buf(name)

        with ExitStack() as es:
            tb, tbB = sbt(es, "tb", [32, 12], F32)
            oh, ohB = sbt(es, "oh", [32, 3, 384], F32)
            val, valB = sbt(es, "val", [4, 384], F32)
            ue, ueB = sbt(es, "ue", [4, 384], F32)
            anti, antiB = sbt(es, "anti", [128, 128], F32)
            mL, mLB = sbt(es, "mL", [128, 256], F32)
            mR, mRB = sbt(es, "mR", [128, 256], F32)
            S.dma("sp", "tb", tb[:], rel_bias, writes=[tbB])
            S.dma("sp", "oh", oh[:], oh_d, writes=[ohB])
            S.dma("sp", "val", val[:], valid_d, writes=[valB])
            S.dma("sp", "anti", anti[:], anti_d, writes=[antiB])
            S.dma("sp", "mL", mL[:], mL_d, writes=[mLB])
            S.dma("sp", "mR", mR[:], mR_d, writes=[mRB])
            zt, ztB = sbt(es, "zt", [128, 4, 8], BF16)
            S.op("dve", lambda e: e.memset(zt[:], 0.0), writes=[ztB])
            S.dma("sp", "zt", xaT_d[:, :, 0:8].rearrange("k p n -> p k n"), zt[:], reads=[ztB], writes=[DB("xa", 0)])
            S.dma("sp", "zt", xaT_d[:, :, NT + 8:NT + 16].rearrange("k p n -> p k n"), zt[:], reads=[ztB], writes=[DB("xa", NTILES - 1)])
            uB = Buf("u_d")
            for g in range(3):
                pt, pB = psnext()
                S.op("pe", lambda e, g=g, pt=pt: e.matmul(pt[0:4, 0:384], lhsT=tb[:, 4 * g:4 * g + 4], rhs=oh[:, g, :],
                                                       start=True, stop=True), reads=[tbB, ohB], writes=[pB])
                S.op("act", lambda e, pt=pt: e.activation(out=ue[:], in_=pt[0:4, 0:384], func=AF.Exp),
                     reads=[pB], writes=[ueB])
                S.op("dve", lambda e: e.tensor_tensor(out=ue[:], in0=ue[:], in1=val[:], op=ALU.mult),
                     reads=[ueB, valB], writes=[ueB])
                S.dma("sp", "ue", u_d[4 * g:4 * g + 4, :], ue[:], reads=[ueB], writes=[uB])
            EdB = Buf("E_d")
            hkl = [sbt(es, "hk", [128, 256], F32) for _ in range(2)]
            evl = [sbt(es, "ev", [128, 3, 256], F32) for _ in range(2)]
            for h in range(12):
                hk, hkB = hkl[h % 2]
                ev, evB = evl[h % 2]
                S.dma("sp", "hk%d" % (h % 2), hk[:], bass.AP(u_d.tensor, h * 384, [[1, 128], [1, 256]]),
                      reads=[uB], writes=[hkB])
                pt, pB = psnext()
                S.op("pe", lambda e, pt=pt, hk=hk: e.matmul(pt[:, 0:256], lhsT=anti[:], rhs=hk[:], start=True, stop=True),
                     reads=[antiB, hkB], writes=[pB])
                S.op("act", lambda e, pt=pt, ev=ev: e.copy(out=ev[:, 0, :], in_=pt[:, 0:256]), reads=[pB], writes=[evB])
                S.op("dve", lambda e, ev=ev: e.tensor_tensor(out=ev[:, 1, :], in0=ev[:, 0, :], in1=mL[:], op=ALU.mult),
                     reads=[evB, mLB], writes=[evB])
                S.op("dve", lambda e, ev=ev: e.tensor_tensor(out=ev[:, 2, :], in0=ev[:, 0, :], in1=mR[:], op=ALU.mult),
                     reads=[evB, mRB], writes=[evB])
                S.dma("sp", "ev%d" % (h % 2), E_d[h].rearrange("v p c -> p v c"), ev[:], reads=[evB], writes=[EdB])
            S.barrier()

        for l in range(L):
            if l > 0:
                S.new_epoch()
            x_src = x_in if l == 0 else xres_d
            last = (l == L - 1)

            if "p1" in cfg.phases:
                with ExitStack() as es:
                    W1, W1B = sbt(es, "W1", [128, 8, 5120], BF16)
                    wsT, wsTB = sbt(es, "wsT", [128, 4, 128], BF16)
                    ident_f, identfB = sbt(es, "identf", [128, 128], F32)
                    ident, identB = sbt(es, "ident", [128, 128], BF16)
                    gbc, gbcB = sbt(es, "gbc", [128, D], F32)
                    lnbf, lnbfB = sbt(es, "lnbf", [128, 512], F32)
                    lnbb, lnbbB = sbt(es, "lnbb", [128, 512], BF16)
                    sgr, sgrB = sbt(es, "sgr", [1, 512], BF16)
                    ones1, ones1B = sbt(es, "ones1", [1, 128], BF16)
                    lng, lngB = sbt(es, "lng", [128, 4], F32)
                    Rt, RtB = sbt(es, "Rt", [128, 4, TT], F32)
                    W1G = {j: j for j in range(10)}
                    W1Bs = [Buf("W1g%d" % i) for i in range(10)]

                    def W1R(col):
                        return W1Bs[W1G[col // 512]]
                    def w1_load(blocks):
                        for j in blocks:
                            S.dma("pool", "W1g%d" % W1G[j], W1[:, :, j * 512:(j + 1) * 512],
                                  w_in[l, :, j * 512:(j + 1) * 512].rearrange("(k p) n -> p k n", p=128), writes=[W1Bs[W1G[j]]])
                    w1_load((3, 2, 4))
                    S.dma("pool", "wsT", wsT[:], sgu_wT[l].rearrange("g q p -> q g p"), writes=[wsTB])
                    S.dma("pool", "sgr", sgr[:], sgub[l:l + 1, :], writes=[sgrB])
                    S.dma("sp", "identf", ident_f[:], ident_d, writes=[identfB])
                    S.dma("sp", "gbc", gbc[:], norm_g[l:l + 1, :].partition_broadcast(128), writes=[gbcB])
                    S.dma("sp", "lnbf", lnbf[:], lnb[l:l + 1, :].partition_broadcast(128), writes=[lnbfB])
                    S.dma("sp", "lng", lng[:], lng_t[l], writes=[lngB])
                    S.op("dve", lambda e: e.tensor_copy(out=ident[:], in_=ident_f[:]), reads=[identfB], writes=[identB])
                    S.op("dve", lambda e: e.tensor_copy(out=lnbb[:], in_=lnbf[:]), reads=[lnbfB], writes=[lnbbB])
                    S.op("dve", lambda e: e.memset(ones1[:], 1.0), writes=[ones1B])
                    for g in range(4):
                        pt, pB = psnext()
                        S.op("pe", lambda e, g=g, pt=pt: e.matmul(pt[:, 0:128], lhsT=lnbb[:, g * 128:(g + 1) * 128],
                                                               rhs=wsT[:, g, :], start=True, stop=False),
                             reads=[lnbbB, wsTB], writes=[pB])
                        S.op("pe", lambda e, g=g, pt=pt: e.matmul(pt[:, 0:128], lhsT=ones1[:, :],
                                                               rhs=sgr[:, g * 128:(g + 1) * 128], start=False, stop=True),
                             reads=[ones1B, sgrB], writes=[pB])
                        for c in range(4):
                            S.op("act", lambda e, g=g, pt=pt, c=c: e.copy(out=Rt[:, g, c * 128:(c + 1) * 128], in_=pt[:, 0:128]),
                                 reads=[pB], writes=[RtB])

                    xs = [sbt(es, "xs", [128, D], F32) for _ in range(4)]
                    xn = [sbt(es, "xn", [128, D], BF16) for _ in range(4)]
                    junk, junkB = sbt(es, "junk", [128, D], BF16)
                    ssq = [sbt(es, "ssq", [128, 2], F32) for _ in range(4)]
                    hT = [sbt(es, "hT", [128, 8, TT], BF16) for _ in range(2)]
                    xa_o, xa_oB = sbt(es, "xa_o", [128, 4, TT], BF16)
                    sga_o, sga_oB = sbt(es, "sga_o", [128, 4, TT], BF16)
                    sgc_o, sgc_oB = sbt(es, "sgc_o", [128, 2, TT], BF16)
                    q_o, q_oB = sbt(es, "q_o", [128, 6, TT], BF16)
                    k_o, k_oB = sbt(es, "k_o", [128, 6, TT], BF16)
                    v_o, v_oB = sbt(es, "v_o", [128, 4, 6, 192], BF16)
                    S.op("pool", lambda e: e.memset(v_o[:], 1.0), writes=[v_oB])
                    u_s, u_sB = sbt(es, "u_s", [128, 4, TT], F32)
                    sgb_s, sgb_sB = sbt(es, "sgb_s", [128, 4, TT], F32)
                    vn = [sbt(es, "vn", [128, 512], BF16) for _ in range(4)]
                    bst = [sbt(es, "bst", [128, 8], F32) for _ in range(4)]
                    t1 = [sbt(es, "t1", [128, TT], F32) for _ in range(2)]
                    t2 = [sbt(es, "t2", [128, TT], F32) for _ in range(2)]
                    bo_o, bo_oB = sbt(es, "bo_o", [128, 4, TT], BF16)

                    def p1_load(t):
                        for c in range(4):
                            xt, xB = xs[c]
                            r0 = t * TT + c * 128
                            S.dma("sp", "xs%d" % c, xt[:], x_src[r0:r0 + 128, :], reads=[DB("x", t)], writes=[xB])

                    evac_rr = [0]

                    def evac_copy(out, in_, reads, writes):
                        evac_rr[0] += 1
                        if evac_rr[0] % 2:
                            S.op("act", lambda e: e.copy(out=out, in_=in_), reads=reads, writes=writes)
                        else:
                            S.op("dve", lambda e: e.tensor_copy(out=out, in_=in_), reads=reads, writes=writes)

                    def p1_norm_elem(t):
                        for c in range(4):
                            xt, xB = xs[c]
                            sq, sqB = ssq[c]
                            xnt, xnB = xn[c]
                            S.op("act", lambda e, xt=xt, sq=sq: e.activation(out=junk[:], in_=xt[:], func=AF.Square,
                                                                             accum_out=sq[:, 0:1]),
                                 reads=[xB], writes=[junkB, sqB])
                            S.op("dve", lambda e, sq=sq: e.tensor_scalar(out=sq[:, 1:2], in0=sq[:, 0:1], scalar1=1.0 / D,
                                                                         scalar2=EPS, op0=ALU.mult, op1=ALU.add),
                                 reads=[sqB], writes=[sqB])
                            S.op("act", lambda e, sq=sq: e.sqrt(out=sq[:, 1:2], in_=sq[:, 1:2]), reads=[sqB], writes=[sqB])
                            S.op("dve", lambda e, sq=sq: e.reciprocal(out=sq[:, 1:2], in_=sq[:, 1:2]), reads=[sqB], writes=[sqB])
                            S.op("dve", lambda e, xt=xt, sq=sq, xnt=xnt: e.scalar_tensor_tensor(
                                out=xnt[:], in0=xt[:], scalar=sq[:, 1:2], in1=gbc[:], op0=ALU.mult, op1=ALU.mult),
                                reads=[xB, sqB, gbcB], writes=[xnB])

                    def p1_norm_pe(t):
                        hTt, hTB = hT[t % 2]
                        tok = slice(t * TT, (t + 1) * TT)
                        for c in range(4):
                            xnt, xnB = xn[c]
                            pt, pB = psnext()
                            ptb = pt[:].bitcast(BF16)
                            for k in range(8):
                                S.op("pe", lambda e, k=k, ptb=ptb, xnt=xnt: e.transpose(
                                    out=ptb[:, k * 128:(k + 1) * 128], in_=xnt[:, k * 128:(k + 1) * 128], identity=ident[:]),
                                    reads=[xnB, identB], writes=[pB])
                            evac_copy(hTt[:, :, c * 128:(c + 1) * 128], ptb[:, 0:1024].rearrange("p (k n) -> p k n", k=8),
                                      [pB], [hTB])
                        S.dma("sp", "hT%d" % (t % 2), hT_d[:, :, tok].rearrange("k p n -> p k n"), hTt[:],
                              reads=[hTB], writes=[DB("hT", t)])

                    p1_load(0)
                    p1_norm_elem(0)
                    if NTILES > 1:
                        p1_load(1)
                    p1_norm_pe(0)
                    for t in range(NTILES):
                        hTt, hTB = hT[t % 2]
                        tok = slice(t * TT, (t + 1) * TT)
                        if t + 1 < NTILES:
                            p1_norm_elem(t + 1)
                            if t + 2 < NTILES:
                                p1_load(t + 2)

                        def proj(col, pt, pB):
                            for k in range(8):
                                S.op("pe", lambda e, k=k: e.matmul(pt[:, :], lhsT=W1[:, k, col:col + 128], rhs=hTt[:, k, :],
                                                                   start=(k == 0), stop=(k == 7)),
                                     reads=[W1R(col), hTB], writes=[pB])

                        for c in range(4):
                            pt, pB = psnext()
                            for k in range(8):
                                S.op("pe", lambda e, k=k, pt=pt, c=c: e.matmul(
                                    pt[:, :], lhsT=hTt[:, k, c * 128:(c + 1) * 128], rhs=W1[:, k, 1536:2048],
                                    start=(k == 0), stop=(k == 7)), reads=[W1R(1536), hTB], writes=[pB])
                            bs, bsB = bst[c]
                            vnt, vnB = vn[c]
                            S.op("dve", lambda e, bs=bs, pt=pt: e.bn_stats(out=bs[:, 0:6], in_=pt[:, :]), reads=[pB], writes=[bsB])
                            S.op("dve", lambda e, bs=bs: e.bn_aggr(out=bs[:, 6:8], in_=bs[:, 0:6]), reads=[bsB], writes=[bsB])
                            S.op("dve", lambda e, bs=bs: e.tensor_scalar(out=bs[:, 7:8], in0=bs[:, 7:8], scalar1=1.0,
                                                                         scalar2=EPS, op0=ALU.mult, op1=ALU.add),
                                 reads=[bsB], writes=[bsB])
                            S.op("act", lambda e, bs=bs: e.sqrt(out=bs[:, 7:8], in_=bs[:, 7:8]), reads=[bsB], writes=[bsB])
                            S.op("dve", lambda e, bs=bs: e.reciprocal(out=bs[:, 7:8], in_=bs[:, 7:8]), reads=[bsB], writes=[bsB])
                            S.op("dve", lambda e, bs=bs, pt=pt, vnt=vnt: e.tensor_scalar(
                                out=vnt[:], in0=pt[:, :], scalar1=bs[:, 6:7], scalar2=bs[:, 7:8], op0=ALU.subtract,
                                op1=ALU.mult), reads=[pB, bsB], writes=[vnB])
                        for g in range(4):
                            pt, pB = psnext()
                            proj(1024 + g * 128, pt, pB)
                            evac_copy(u_s[:, g, :], pt[:, :], [pB], [u_sB])
                            pt, pB = psnext()
                            proj(2048 + g * 128, pt, pB)
                            S.op("act", lambda e, g=g, pt=pt: e.activation(out=sgb_s[:, g, :], in_=pt[:, :], func=AF.Silu),
                                 reads=[pB], writes=[sgb_sB])
                        for g in range(4):
                            pt, pB = psnext()
                            for c in range(4):
                                vnt, vnB = vn[c]
                                S.op("pe", lambda e, g=g, c=c, pt=pt, vnt=vnt: e.matmul(
                                    pt[:, c * 128:(c + 1) * 128], lhsT=vnt[:, g * 128:(g + 1) * 128], rhs=wsT[:, g, :],
                                    start=True, stop=True), reads=[vnB, wsTB], writes=[pB])
                            a1, a1B = t1[g % 2]
                            a2, a2B = t2[g % 2]
                            S.op("dve", lambda e, g=g, pt=pt, a1=a1: e.scalar_tensor_tensor(
                                out=a1[:], in0=pt[:, :], scalar=lng[:, g:g + 1], in1=Rt[:, g, :],
                                op0=ALU.mult, op1=ALU.add), reads=[pB, lngB, RtB], writes=[a1B])
                            S.op("pool", lambda e, g=g, a2=a2: e.tensor_tensor(out=a2[:], in0=u_s[:, g, :], in1=sgb_s[:, g, :],
                                                                               op=ALU.mult),
                                 reads=[u_sB, sgb_sB], writes=[a2B])
                            S.op("pool", lambda e, g=g, a1=a1, a2=a2: e.tensor_tensor(out=bo_o[:, g, :], in0=a1[:], in1=a2[:],
                                                                                      op=ALU.mult),
                                 reads=[a1B, a2B], writes=[bo_oB])
                        S.dma("sp", "bo_o", boutT_d[:, :, tok].rearrange("k p n -> p k n"), bo_o[:],
                              reads=[bo_oB], writes=[DB("bout", t)])
                        if t == 0:
                            w1_load((0, 1, 5, 6, 7, 8, 9))
                        if t + 1 < NTILES:
                            p1_norm_pe(t + 1)
                        for g in range(4):
                            pt, pB = psnext()
                            proj(g * 128, pt, pB)
                            evac_copy(xa_o[:, g, :], pt[:, :], [pB], [xa_oB])
                        S.dma("sp", "xa_o", xaT_d[:, :, 8 + t * TT:8 + (t + 1) * TT].rearrange("k p n -> p k n"), xa_o[:],
                              reads=[xa_oB], writes=[DB("xa", t)])
                        for g in range(4):
                            pt, pB = psnext()
                            proj(512 + g * 128, pt, pB)
                            S.op("act", lambda e, g=g, pt=pt: e.activation(out=sga_o[:, g, :], in_=pt[:, :], func=AF.Silu),
                                 reads=[pB], writes=[sga_oB])
                        S.dma("sp", "sga_o", sgaT_d[:, :, tok].rearrange("k p n -> p k n"), sga_o[:],
                              reads=[sga_oB], writes=[DB("sga", t)])
                        for j in range(6):
                            pt, pB = psnext()
                            proj(2560 + j * 128, pt, pB)
                            evac_copy(q_o[:, j, :], pt[:, :], [pB], [q_oB])
                        S.dma("sp", "q_o", qT_d[:, :, tok].rearrange("k p n -> p k n"), q_o[:],
                              reads=[q_oB], writes=[DB("q", t)])
                        for j in range(6):
                            pt, pB = psnext()
                            proj(3328 + j * 128, pt, pB)
                            evac_copy(k_o[:, j, :], pt[:, :], [pB], [k_oB])
                        S.dma("sp", "k_o", kT_d[:, :, tok].rearrange("k p n -> p k n"), k_o[:],
                              reads=[k_oB], writes=[DB("k", t)])
                        for c in range(4):
                            for (c0, cw) in ((0, 512), (512, 256)):
                                pt, pB = psnext()
                                for k in range(8):
                                    S.op("pe", lambda e, k=k, pt=pt, c=c, c0=c0, cw=cw: e.matmul(
                                        pt[:, 0:cw], lhsT=hTt[:, k, c * 128:(c + 1) * 128],
                                        rhs=W1[:, k, 4096 + c0:4096 + c0 + cw], start=(k == 0), stop=(k == 7)),
                                        reads=[W1R(4096 + c0), hTB], writes=[pB])
                                ch0, nch = c0 // 128, cw // 128
                                for X in range(2):
                                    evac_copy(v_o[:, c, ch0:ch0 + nch, X * 128:X * 128 + 64],
                                              pt[:, 0:cw].rearrange("p (h b j) -> p h b j", b=2, j=64)[:, :, X, :], [pB], [v_oB])
                        S.dma("sp", "v_o", v_d[tok, :].rearrange("(c p) f -> p c f", p=128), v_o[:].rearrange("p c h f -> p c (h f)"),
                              reads=[v_oB], writes=[DB("v", t)])
                        for j in range(2):
                            pt, pB = psnext()
                            proj(4864 + j * 128, pt, pB)
                            S.op("act", lambda e, j=j, pt=pt: e.activation(out=sgc_o[:, j, :], in_=pt[:, :], func=AF.Silu),
                                 reads=[pB], writes=[sgc_oB])
                        S.dma("sp", "sgc_o", sgcT_d[:, :, tok].rearrange("k p n -> p k n"), sgc_o[:],
                              reads=[sgc_oB], writes=[DB("sgc", t)])
                    S.barrier()

            if "pa" in cfg.phases:
                with ExitStack() as es:
                    SEGM = 2 * U
                    FC = 2048
                    NFC = SEGM // FC
                    q_s, q_sB = sbt(es, "q_s", [128, SEGM], BF16)
                    k_s, k_sB = sbt(es, "k_s", [128, SEGM], BF16)
                    qp, qpB = sbt(es, "qp", [128, SEGM], BF16)
                    kp, kpB = sbt(es, "kp", [128, SEGM], BF16)
                    v_s, _ = sbt(es, "v_s", [128, SEGM // 128, 192], BF16)
                    v_hB = [Buf("v_h0"), Buf("v_h1")]
                    accA, _ = sbt(es, "accA", [128, SEGM], F32)
                    accB, _ = sbt(es, "accB", [128, SEGM], F32)
                    accAB = [Buf("accA%d" % i) for i in range(NFC)]
                    accBB = [Buf("accB%d" % i) for i in range(NFC)]
                    Etl = [sbt(es, "Et", [128, 2, 3, 256], F32) for _ in range(2)]
                    swA, swAB = sbt(es, "swA", [128, 128], F32)
                    swB, swBB = sbt(es, "swB", [128, 128], F32)
                    er = [sbt(es, "er", [128, 256], F32) for _ in range(4)]
                    pr = [sbt(es, "pr", [128, 256], BF16) for _ in range(6)]
                    rec = [sbt(es, "rec", [128, 512], F32) for _ in range(2)]
                    cq = [sbt(es, "cq", [128, 512], F32) for _ in range(2)]
                    sgc_c = [sbt(es, "sgc_c", [128, FC], BF16) for _ in range(2)]
                    co_c = [sbt(es, "co_c", [128, FC], BF16) for _ in range(2)]
                    S.dma("sp", "swA", swA[:], swA_d, writes=[swAB])
                    S.dma("sp", "swB", swB[:], swB_d, writes=[swBB])
                    ring[0] = 0
                    nd_rr = [0]
                    combos = []
                    for (s0, slen) in cfg.segs:
                        for hp in range(2):
                            for g in range(3):
                                combos.append((s0, slen, hp, g))

                    def pa_load(ci):
                        s0, slen, hp, g = combos[ci]
                        d = DILS[g]
                        ch = 2 * g + hp
                        t0, t1_ = s0 // TT, (s0 + slen) // TT
                        Et, EtB = Etl[ci % 2]
                        S.dma("sp", "q_s", q_s[:, 0:slen], qT_d[ch, :, s0:s0 + slen], reads=DBr("q", t0, t1_), writes=[q_sB])
                        S.dma("sp", "k_s", k_s[:, 0:slen], kT_d[ch, :, s0:s0 + slen], reads=DBr("k", t0, t1_), writes=[k_sB])
                        S.dma("sp", "Et%d" % (ci % 2), Et[:], E_d[4 * g + 2 * hp:4 * g + 2 * hp + 2].rearrange("h v p c -> p h v c"),
                              writes=[EtB])

                    def pa_load_v(ci, half):
                        s0, slen, hp, g = combos[ci]
                        d = DILS[g]
                        ch = 2 * g + hp
                        t0, t1_ = s0 // TT, (s0 + slen) // TT
                        ntr = slen // d // 128
                        vsrc = v_d[s0:s0 + slen, ch * 192:(ch + 1) * 192].rearrange("(jj p r) f -> r p jj f", p=128, r=d)
                        if d == 1:
                            j0, j1 = half * (ntr // 2), (half + 1) * (ntr // 2)
                            S.dma("sp", "v_h%d" % half, v_s[:, j0:j1, :], vsrc[0][:, j0:j1, :], reads=DBr("v", t0, t1_), writes=[v_hB[half]])
                        else:
                            for r in range(half * (d // 2), (half + 1) * (d // 2)):
                                S.dma("sp", "v_h%d" % half, v_s[:, r * ntr:(r + 1) * ntr, :], vsrc[r], reads=DBr("v", t0, t1_), writes=[v_hB[half]])

                    def pa_permute(ci):
                        s0, slen, hp, g = combos[ci]
                        d = DILS[g]
                        Lr = slen // d
                        nsp = 4
                        for (src, srcB, dst, dstB) in ((q_s, q_sB, qp, qpB), (k_s, k_sB, kp, kpB)):
                            for hh in range(nsp):
                                ls = slice(hh * (Lr // nsp), (hh + 1) * (Lr // nsp))
                                if d == 1:
                                    S.op("act", lambda e, src=src, dst=dst, ls=ls: e.copy(out=dst[:, ls], in_=src[:, ls]),
                                         reads=[srcB], writes=[dstB])
                                else:
                                    S.op("act", lambda e, src=src, dst=dst, ls=ls: e.copy(
                                        out=dst[:, 0:slen].rearrange("p (r l) -> p r l", r=d)[:, :, ls],
                                        in_=src[:, 0:slen].rearrange("p (l r) -> p r l", r=d)[:, :, ls]),
                                        reads=[srcB], writes=[dstB])

                    pa_load(0)
                    pa_load_v(0, 0)
                    pa_load_v(0, 1)
                    pa_permute(0)
                    for ci, (s0, slen, hp, g) in enumerate(combos):
                        if ci + 1 < len(combos):
                            pa_load(ci + 1)
                        d = DILS[g]
                        Lr = slen // d
                        nblk = Lr // 64
                        ntr = nblk // 2
                        mid = nblk // 2 if slen == 2 * U else -1
                        nfc = slen // FC
                        Et, EtB = Etl[ci % 2]
                        tiles = []
                        for r in range(d):
                            for jj in range(ntr):
                                for X in range(2):
                                    tiles.append((r, jj, X))
                        n = len(tiles)
                        st = {}
                        banks = {}

                        def get_bank(r, b):
                            key = (r, b)
                            if key not in banks:
                                i = nd_rr[0] % 2
                                nd_rr[0] += 1
                                banks[key] = (4 + 2 * i, 5 + 2 * i, set())
                            return banks[key]

                        def geom(r, jj):
                            qlo = max(0, 2 * jj - 1)
                            qhi = min(nblk, 2 * jj + 3)
                            c0 = 64 * (qlo - (2 * jj - 1))
                            if 2 * jj == mid:
                                var = 1
                            elif 2 * jj + 2 == mid:
                                var = 2
                            else:
                                var = 0
                            return qlo, qhi, c0, var

                        LAG = 4
                        sched = []
                        for i0 in range(0, n + LAG, 2):
                            for ii in (i0, i0 + 1):
                                if ii < n:
                                    sched.append(("f", ii))
                            for ii in (i0, i0 + 1):
                                if 0 <= ii - LAG < n:
                                    sched.append(("b", ii - LAG))
                        assert len(sched) == 2 * n
                        for (kind, idx) in sched:
                            i = idx if kind == "f" else n
                            if i < n:
                                r, jj, X = tiles[i]
                                qlo, qhi, c0, var = geom(r, jj)
                                ncol = 64 * (qhi - qlo)
                                pt, pB = psnext(0, 4)
                                K0 = r * Lr + 128 * jj
                                Q0 = r * Lr + 64 * qlo
                                S.op("pe", lambda e, pt=pt, X=X, K0=K0, Q0=Q0, ncol=ncol: e.matmul(
                                    pt[:, 0:ncol], lhsT=kp[X * 64:(X + 1) * 64, K0:K0 + 128],
                                    rhs=qp[X * 64:(X + 1) * 64, Q0:Q0 + ncol], start=True, stop=True),
                                    reads=[kpB, qpB], writes=[pB])
                                et, etB = er[i % 4]
                                ptl, ptlB = pr[i % 6]
                                S.op("act", lambda e, pt=pt, et=et, ncol=ncol: e.activation(
                                    out=et[:, 0:ncol], in_=pt[:, 0:ncol], func=AF.Exp, scale=0.125),
                                    reads=[pB], writes=[etB])
                                meng = "dve" if i % 3 == 0 else "pool"
                                S.op(meng, lambda e, et=et, ptl=ptl, X=X, var=var, c0=c0, ncol=ncol: e.tensor_tensor(
                                    out=ptl[:, 0:ncol], in0=et[:, 0:ncol], in1=Et[:, X, var, c0:c0 + ncol], op=ALU.mult),
                                    reads=[etB, EtB], writes=[ptlB])
                            j = idx if kind == "b" else -1
                            if 0 <= j < n:
                                r, jj, X = tiles[j]
                                qlo, qhi, c0, var = geom(r, jj)
                                ptl, ptlB = pr[j % 6]
                                blk = qlo
                                while blk < qhi:
                                    b = blk // 8
                                    bend = min(qhi, (b + 1) * 8)
                                    bkA, bkB, started = get_bank(r, b)
                                    bk = bkA if X == 0 else bkB
                                    first = X not in started
                                    started.add(X)
                                    oc = 64 * (blk - 8 * b)
                                    w = 64 * (bend - blk)
                                    pc = 64 * (blk - qlo)
                                    S.op("pe", lambda e, bk=bk, X=X, oc=oc, w=w, pc=pc, ptl=ptl, r=r, jj=jj, first=first: e.matmul(
                                        ps[bk][:, oc:oc + w], lhsT=v_s[:, r * ntr + jj, X * 64:X * 64 + 128],
                                        rhs=ptl[:, pc:pc + w], start=first, stop=False, skip_group_check=True),
                                        reads=[v_hB[0 if (r * ntr + jj) < (d * ntr) // 2 else 1], ptlB], writes=[psb[bk]])
                                    blk = bend
                                if X == 1:
                                    done = [(rr, b) for (rr, b) in list(banks.keys()) if rr == r and jj == min(ntr - 1, 4 * b + 4)]
                                    for (rr, b) in done:
                                        bkA, bkB, _ = banks.pop((rr, b))
                                        nb = min(8, nblk - 8 * b) * 64
                                        l0 = 512 * b
                                        for (acc, accBufs, bk, eng0) in ((accA, accAB, bkA, "act"), (accB, accBB, bkB, "dve")):
                                            if d == 1:
                                                ov = acc[:, l0:l0 + nb]
                                                bl = [accBufs[l0 // FC]]
                                            else:
                                                ov = acc[:, 0:slen].rearrange("p (l r) -> p r l", r=d)[:, rr, l0:l0 + nb]
                                                bl = accBufs[0:nfc]
                                            if g == 0:
                                                if eng0 == "act":
                                                    S.op("act", lambda e, ov=ov, bk=bk, nb=nb: e.copy(out=ov, in_=ps[bk][:, 0:nb]),
                                                         reads=[psb[bk]], writes=bl)
                                                else:
                                                    S.op("dve", lambda e, ov=ov, bk=bk, nb=nb: e.tensor_copy(out=ov, in_=ps[bk][:, 0:nb]),
                                                         reads=[psb[bk]], writes=bl)
                                            else:
                                                S.op("dve", lambda e, ov=ov, bk=bk, nb=nb: e.tensor_tensor(
                                                    out=ov, in0=ps[bk][:, 0:nb], in1=ov, op=ALU.add),
                                                    reads=[psb[bk]] + bl, writes=bl)
                            if j == n // 2 and ci + 1 < len(combos):
                                pa_load_v(ci + 1, 0)
                        assert not banks, banks
                        if ci + 1 < len(combos):
                            pa_load_v(ci + 1, 1)
                            pa_permute(ci + 1)
                        if g == 2:
                            for fc in range(nfc):
                                sg, sgB = sgc_c[fc % 2]
                                co, coB = co_c[fc % 2]
                                tf0 = (s0 + fc * FC) // TT
                                S.dma("sp", "sgc_c%d" % (fc % 2), sg[:], sgcT_d[hp, :, s0 + fc * FC:s0 + (fc + 1) * FC],
                                      reads=DBr("sgc", tf0, tf0 + FC // TT), writes=[sgB])
                                for pc_ in range(FC // 512):
                                    cs = slice(fc * FC + pc_ * 512, fc * FC + (pc_ + 1) * 512)
                                    ls_ = slice(pc_ * 512, (pc_ + 1) * 512)
                                    pt, pB = psnext(0, 4)
                                    S.op("pe", lambda e, pt=pt, cs=cs: e.matmul(pt[:, :], lhsT=swA[:, :], rhs=accA[:, cs], start=True, stop=False),
                                         reads=[swAB, accAB[fc]], writes=[pB])
                                    S.op("pe", lambda e, pt=pt, cs=cs: e.matmul(pt[:, :], lhsT=swB[:, :], rhs=accB[:, cs], start=False, stop=True),
                                         reads=[swBB, accBB[fc]], writes=[pB])
                                    rc, rcB = rec[pc_ % 2]
                                    cqt, cqB = cq[pc_ % 2]
                                    S.op("dve", lambda e, rc=rc, pt=pt: e.reciprocal(out=rc[:], in_=pt[:, :]), reads=[pB], writes=[rcB])
                                    S.op("pool", lambda e, rc=rc, cqt=cqt, cs=cs: e.tensor_tensor(
                                        out=cqt[0:64, :], in0=accA[0:64, cs], in1=rc[0:64, :], op=ALU.mult),
                                        reads=[accAB[fc], rcB], writes=[cqB])
                                    S.op("pool", lambda e, rc=rc, cqt=cqt, cs=cs: e.tensor_tensor(
                                        out=cqt[64:128, :], in0=accB[64:128, cs], in1=rc[64:128, :], op=ALU.mult),
                                        reads=[accBB[fc], rcB], writes=[cqB])
                                    S.op("pool", lambda e, cqt=cqt, sg=sg, co=co, ls_=ls_: e.tensor_tensor(
                                        out=co[:, ls_], in0=cqt[:], in1=sg[:, ls_], op=ALU.mult),
                                        reads=[cqB, sgB], writes=[coB])
                                S.dma("sp", "co_c%d" % (fc % 2), coutT_d[hp, :, s0 + fc * FC:s0 + (fc + 1) * FC], co[:],
                                      reads=[coB], writes=DBr("cout", tf0, tf0 + FC // TT))
                    S.barrier()

            if "p2" in cfg.phases:
                with ExitStack() as es:
                    Wmg, WmgB = sbt(es, "Wmg", [128, 8, 3072], BF16)
                    Wa, WaB = sbt(es, "Wa", [128, 4, D], BF16)
                    Wb, WbB = sbt(es, "Wb", [128, 4, D], BF16)
                    Wc, WcB = sbt(es, "Wc", [128, 2, D], BF16)
                    Wo, WoB = sbt(es, "Wo", [128, 8, D], BF16)
                    pw, pwB = sbt(es, "pw", [128, 4, 128], BF16)
                    psc, pscB = sbt(es, "psc", [128, 4], F32)
                    flg, flgB = sbt(es, "flg", [128, 1], F32)
                    fgb, fgbB = sbt(es, "fgb", [128, D], F32)
                    WmgBs = [Buf("Wmg%d" % i) for i in range(6)]
                    S.dma("pool", "pw", pw[:], pool_w[l].rearrange("g c d -> c g d"), writes=[pwB])
                    for j in (0, 2, 4):
                        S.dma("pool", "Wmg%d" % j, Wmg[:, :, j * 512:(j + 1) * 512],
                              w_in[l, :, 5120 + j * 512:5120 + (j + 1) * 512].rearrange("(k p) n -> p k n", p=128), writes=[WmgBs[j]])
                    S.dma("pool", "Wa", Wa[:], w_bra[l].rearrange("(k p) n -> p k n", p=128), writes=[WaB])
                    S.dma("pool", "Wb", Wb[:], w_brb[l].rearrange("(k p) n -> p k n", p=128), writes=[WbB])
                    S.dma("pool", "Wc", Wc[:], w_brc[l].rearrange("(k p) n -> p k n", p=128), writes=[WcB])
                    def w2_late():
                        for j in (1, 3, 5):
                            S.dma("pool", "Wmg%d" % j, Wmg[:, :, j * 512:(j + 1) * 512],
                                  w_in[l, :, 5120 + j * 512:5120 + (j + 1) * 512].rearrange("(k p) n -> p k n", p=128), writes=[WmgBs[j]])
                        for j in range(2):
                            S.dma("pool", "Wo", Wo[:, :, j * 512:(j + 1) * 512],
                                  w_out[l, :, j * 512:(j + 1) * 512].rearrange("(k p) n -> p k n", p=128), writes=[WoB])
                    S.dma("sp", "psc", psc[:], pscale_t[l], writes=[pscB])
                    S.dma("sp", "flg", flg[:], flag_d, writes=[flgB])
                    if last:
                        S.dma("sp", "fgb", fgb[:], final_g[0:1, :].partition_broadcast(128), writes=[fgbB])

                    hT2 = [sbt(es, "hT2", [128, 8, TT], BF16) for _ in range(2)]
                    xs2 = [sbt(es, "xs2", [128, D], F32) for _ in range(4)]
                    xa2 = [sbt(es, "xa2", [128, 4, TT + 16], BF16) for _ in range(2)]
                    sga2 = [sbt(es, "sga2", [128, 4, TT], BF16) for _ in range(2)]
                    bo2 = [sbt(es, "bo2", [128, 4, TT], BF16) for _ in range(2)]
                    co2 = [sbt(es, "co2", [128, 2, TT], BF16) for _ in range(2)]
                    pinv, pinvB = sbt(es, "pinv", [128, 4, TT], F32)
                    sA, sAB = sbt(es, "sA", [128, TT + 16], F32)
                    sB_, sBB = sbt(es, "sB", [128, TT + 16], F32)
                    sC, sCB = sbt(es, "sC", [128, TT + 16], F32)
                    mx = [sbt(es, "mx", [128, TT], BF16) for _ in range(4)]
                    aol = [sbt(es, "ao", [128, 4, TT], BF16) for _ in range(2)]
                    sg = [sbt(es, "sg", [128, TT], F32) for _ in range(4)]
                    tm = [sbt(es, "tm", [128, TT], F32) for _ in range(4)]
                    mg, mgB = sbt(es, "mg", [128, 8, TT], BF16)
                    junk2, junk2B = sbt(es, "junk2", [128, D], BF16)
                    ss2 = [sbt(es, "ss2", [128, 2], F32) for _ in range(4)]
                    ring[0] = 0

                    bnd = {0: (0, "L", "hard"), TPU - 1: (1, "R", "flag"), TPU: (2, "L", "flag"), 2 * TPU - 1: (3, "R", "hard"),
                           2 * TPU: (4, "L", "hard"), 3 * TPU - 1: (5, "R", "hard")}

                    def p2_load(t):
                        s = t % 2
                        tok = slice(t * TT, (t + 1) * TT)
                        S.dma("sp", "hT2_%d" % s, hT2[s][0][:], hT_d[:, :, tok].rearrange("k p n -> p k n"),
                              reads=[DB("hT", t)], writes=[hT2[s][1]])
                        S.dma("sp", "xa2_%d" % s, xa2[s][0][:], xaT_d[:, :, t * TT:t * TT + TT + 16].rearrange("k p n -> p k n"),
                              reads=[DB("xa", tt) for tt in (t - 1, t, t + 1) if 0 <= tt < NTILES], writes=[xa2[s][1]])
                        S.dma("sp", "sga2_%d" % s, sga2[s][0][:], sgaT_d[:, :, tok].rearrange("k p n -> p k n"),
                              reads=[DB("sga", t)], writes=[sga2[s][1]])
                        S.dma("sp", "bo2_%d" % s, bo2[s][0][:], boutT_d[:, :, tok].rearrange("k p n -> p k n"),
                              reads=[DB("bout", t)], writes=[bo2[s][1]])
                        S.dma("sp", "co2_%d" % s, co2[s][0][:], coutT_d[:, :, tok].rearrange("k p n -> p k n"),
                              reads=[DB("cout", t)], writes=[co2[s][1]])

                    def p2_pool_elem(t, groups=(0, 1, 2, 3)):
                        s = t % 2
                        xat, xaB = xa2[s]
                        binfo = bnd.get(t)
                        if binfo is not None and 0 in groups:
                            bi, side, kind = binfo
                            S.dma("sp", "pinv", pinv[:].rearrange("p g n -> p (g n)"),
                                  pinv_d[bi:bi + 1].rearrange("o g n -> o (g n)").partition_broadcast(128), writes=[pinvB])
                            hs = slice(0, 8) if side == "L" else slice(TT + 8, TT + 16)
                            if kind == "hard":
                                S.op("pool", lambda e, xat=xat, hs=hs: e.memset(xat[:, :, hs], 0.0), reads=[xaB], writes=[xaB])
                            else:
                                S.op("pool", lambda e, xat=xat, hs=hs: e.tensor_scalar(
                                    out=xat[:, :, hs], in0=xat[:, :, hs], scalar1=flg[:, 0:1], scalar2=None, op0=ALU.mult),
                                    reads=[xaB, flgB], writes=[xaB])
                        NW = TT + 16
                        for g in groups:
                            w = POOL_WINDOWS[g]
                            xg = xat[:, g, :]
                            S.op("pool", lambda e, xg=xg: e.tensor_tensor(out=sA[:, 1:NW], in0=xg[:, 0:NW - 1], in1=xg[:, 1:NW], op=ALU.add),
                                 reads=[xaB], writes=[sAB])
                            cur, curB = sA, sAB
                            if w >= 4:
                                S.op("pool", lambda e: e.tensor_tensor(out=sB_[:, 2:NW - 1], in0=sA[:, 1:NW - 2], in1=sA[:, 3:NW], op=ALU.add),
                                     reads=[sAB], writes=[sBB])
                                cur, curB = sB_, sBB
                            if w >= 8:
                                S.op("pool", lambda e: e.tensor_tensor(out=sC[:, 4:NW - 3], in0=sB_[:, 2:NW - 5], in1=sB_[:, 6:NW - 1], op=ALU.add),
                                     reads=[sBB], writes=[sCB])
                                cur, curB = sC, sCB
                            if w >= 16:
                                S.op("pool", lambda e: e.tensor_tensor(out=sA[:, 8:NW - 7], in0=sC[:, 4:NW - 11], in1=sC[:, 12:NW - 3], op=ALU.add),
                                     reads=[sCB], writes=[sAB])
                                cur, curB = sA, sAB
                            mxt, mxB = mx[g]
                            if binfo is None:
                                S.op("dve", lambda e, cur=cur, xg=xg, mxt=mxt, w=w: e.scalar_tensor_tensor(
                                    out=mxt[:], in0=cur[:, 8:8 + TT], scalar=1.0 / w, in1=xg[:, 8:8 + TT], op0=ALU.mult, op1=ALU.subtract),
                                    reads=[curB, xaB], writes=[mxB])
                            else:
                                S.op("pool", lambda e, cur=cur, g=g: e.tensor_tensor(out=cur[:, 8:8 + TT], in0=cur[:, 8:8 + TT], in1=pinv[:, g, :], op=ALU.mult),
                                     reads=[curB, pinvB], writes=[curB])
                                S.op("pool", lambda e, cur=cur, xg=xg, mxt=mxt: e.tensor_tensor(out=mxt[:], in0=cur[:, 8:8 + TT], in1=xg[:, 8:8 + TT], op=ALU.subtract),
                                     reads=[curB, xaB], writes=[mxB])

                    def p2_pool_mm(t):
                        s = t % 2
                        sgat, sgaB = sga2[s]
                        aot, aotB = aol[s]
                        for g in range(4):
                            mxt, mxB = mx[g]
                            pt, pB = psnext()
                            S.op("pe", lambda e, g=g, pt=pt, mxt=mxt: e.matmul(pt[:, :], lhsT=pw[:, g, :], rhs=mxt[:], start=True, stop=True),
                                 reads=[pwB, mxB], writes=[pB])
                            S.op("dve", lambda e, g=g, pt=pt, sgat=sgat, aot=aot: e.scalar_tensor_tensor(
                                out=aot[:, g, :], in0=pt[:, :], scalar=psc[:, g:g + 1], in1=sgat[:, g, :], op0=ALU.mult, op1=ALU.mult),
                                reads=[pB, pscB, sgaB], writes=[aotB])

                    pre_g = [None]

                    def emit_gates(tt, m):
                        hTg, hTgB = hT2[tt % 2]
                        out = []
                        for bi_ in range(3):
                            pg, pgB = psnext()
                            col = bi_ * D + m * 128
                            for k in range(8):
                                S.op("pe", lambda e, k=k, pg=pg, col=col: e.matmul(pg[:, :], lhsT=Wmg[:, k, col:col + 128], rhs=hTg[:, k, :],
                                                                                  start=(k == 0), stop=(k == 7)),
                                     reads=[WmgBs[col // 512], hTgB], writes=[pgB])
                            idx = (m * 3 + bi_) % 4
                            sgt, sgB = sg[idx]
                            S.op("act", lambda e, pg=pg, sgt=sgt: e.activation(out=sgt[:], in_=pg[:, :], func=AF.Sigmoid),
                                 reads=[pgB], writes=[sgB])
                            out.append((sgt, sgB, idx))
                        return out

                    p2_load(0)
                    p2_pool_elem(0)
                    p2_pool_mm(0)
                    for t in range(NTILES):
                        s = t % 2
                        hTt, hTB = hT2[s]
                        bot, boB = bo2[s]
                        cot, coB = co2[s]
                        ao, aoB = aol[s]
                        for c in range(4):
                            xt, xB = xs2[c]
                            r0 = t * TT + c * 128
                            S.dma("sp", "xs2_%d" % c, xt[:], x_src[r0:r0 + 128, :], reads=[DB("x", t)], writes=[xB])
                        if t + 1 < NTILES:
                            p2_load(t + 1)
                        brs = ((Wa, WaB, 4, ao, aoB), (Wb, WbB, 4, bot, boB), (Wc, WcB, 2, cot, coB))
                        for m in range(8):
                            if m == 0 and pre_g[0] is not None:
                                G = pre_g[0]
                                pre_g[0] = None
                            else:
                                G = emit_gates(t, m)
                            prods = []
                            for bi_, (Wbr, WbrB, nk, src, srcB) in enumerate(brs):
                                pb_, pbB = psnext()
                                for k in range(nk):
                                    S.op("pe", lambda e, k=k, pb_=pb_, Wbr=Wbr, src=src, nk=nk: e.matmul(
                                        pb_[:, :], lhsT=Wbr[:, k, m * 128:(m + 1) * 128], rhs=src[:, k, :], start=(k == 0), stop=(k == nk - 1)),
                                        reads=[WbrB, srcB], writes=[pbB])
                                sgt, sgB, idx = G[bi_]
                                tmt, tmB = tm[idx]
                                S.op("dve", lambda e, pb_=pb_, sgt=sgt, tmt=tmt: e.tensor_tensor(out=tmt[:], in0=pb_[:, :], in1=sgt[:], op=ALU.mult),
                                     reads=[pbB, sgB], writes=[tmB])
                                prods.append((tmt, tmB))
                            (ta, taB), (tb_, tbB_), (tc, tcB) = prods
                            S.op("pool", lambda e, ta=ta, tb_=tb_: e.tensor_tensor(out=ta[:], in0=ta[:], in1=tb_[:], op=ALU.add),
                                 reads=[taB, tbB_], writes=[taB])
                            S.op("pool", lambda e, ta=ta, tc=tc, m=m: e.tensor_tensor(out=mg[:, m, :], in0=ta[:], in1=tc[:], op=ALU.add),
                                 reads=[taB, tcB], writes=[mgB])
                            if t + 1 < NTILES and m < 4:
                                p2_pool_elem(t + 1, groups=(m,))
                            if t == 0 and m == 1:
                                w2_late()
                        if t + 1 < NTILES:
                            pre_g[0] = emit_gates(t + 1, 0)
                            p2_pool_mm(t + 1)
                        for c in range(4):
                            xt, xB = xs2[c]
                            for hf in range(2):
                                pt, pB = psnext()
                                for k in range(8):
                                    S.op("pe", lambda e, k=k, pt=pt, c=c, hf=hf: e.matmul(
                                        pt[:, :], lhsT=mg[:, k, c * 128:(c + 1) * 128], rhs=Wo[:, k, hf * 512:(hf + 1) * 512],
                                        start=(k == 0), stop=(k == 7)), reads=[mgB, WoB], writes=[pB])
                                S.op("dve", lambda e, pt=pt, xt=xt, hf=hf: e.tensor_tensor(
                                    out=xt[:, hf * 512:(hf + 1) * 512], in0=pt[:, :], in1=xt[:, hf * 512:(hf + 1) * 512], op=ALU.add),
                                    reads=[pB, xB], writes=[xB])
                            r0 = t * TT + c * 128
                            if not last:
                                S.dma("sp", "xs2_%d" % c, xres_d[r0:r0 + 128, :], xt[:], reads=[xB], writes=[DB("x", t)])
                            else:
                                sq, sqB = ss2[c]
                                S.op("act", lambda e, xt=xt, sq=sq: e.activation(out=junk2[:], in_=xt[:], func=AF.Square, accum_out=sq[:, 0:1]),
                                     reads=[xB], writes=[junk2B, sqB])
                                S.op("dve", lambda e, sq=sq: e.tensor_scalar(out=sq[:, 1:2], in0=sq[:, 0:1], scalar1=1.0 / D, scalar2=EPS,
                                                                             op0=ALU.mult, op1=ALU.add), reads=[sqB], writes=[sqB])
                                S.op("act", lambda e, sq=sq: e.sqrt(out=sq[:, 1:2], in_=sq[:, 1:2]), reads=[sqB], writes=[sqB])
                                S.op("dve", lambda e, sq=sq: e.reciprocal(out=sq[:, 1:2], in_=sq[:, 1:2]), reads=[sqB], writes=[sqB])
                                S.op("dve", lambda e, xt=xt, sq=sq: e.scalar_tensor_tensor(
                                    out=xt[:], in0=xt[:], scalar=sq[:, 1:2], in1=fgb[:], op0=ALU.mult, op1=ALU.mult),
                                    reads=[xB, sqB, fgbB], writes=[xB])
                                S.dma("sp", "xs2_%d" % c, y_d[r0:r0 + 128, :], xt[:], reads=[xB], writes=[DB("y", t)])
                    S.barrier()
        S.barrier()
    return nc, S


def _bf(a):
    return a


def host_consts(unit):
    ident = np.eye(128, dtype=np.float32)
    anti = np.ascontiguousarray(ident[::-1])
    oh = np.zeros((32, 3, 384), np.float32)
    m = np.arange(384)
    rel = 191 - m
    for g, d in enumerate(DILS):
        b = _t5_bucket(rel * d)
        oh[b, g, m] = 1.0
    valid = (np.abs(rel) <= 64).astype(np.float32)
    valid = np.ascontiguousarray(np.broadcast_to(valid[None, :], (4, 384)))
    return ident, anti, oh, valid


def swap_consts():
    swA = np.zeros((128, 128), np.float32)
    swB = np.zeros((128, 128), np.float32)
    for m in range(64):
        swA[m + 64, m] = 1.0
        swB[m, m + 64] = 1.0
    return {"swA": swA, "swB": swB}


def core_consts(unit, linked):
    flag = np.full((128, 1), 1.0 if linked else 0.0, np.float32)
    mL = np.ones((128, 256), np.float32)
    mR = np.ones((128, 256), np.float32)
    if not linked:
        mL[0:64, 0:64] = 0.0
        mR[64:128, 192:256] = 0.0
    seqs = [(0, 2 * unit)] if linked else [(0, unit), (unit, unit)]
    seqs.append((2 * unit, unit))
    nt = 3 * unit
    inv = np.zeros((4, nt), np.float32)
    for (s0, sl) in seqs:
        t = np.arange(sl)
        for gi, w in enumerate(POOL_WINDOWS):
            lo = np.clip(t - w // 2, 0, sl - 1)
            hi = np.clip(t + w // 2 - 1, 0, sl - 1)
            inv[gi, s0:s0 + sl] = 1.0 / (hi - lo + 1)
    tpu = unit // TT
    btiles = [0, tpu - 1, tpu, 2 * tpu - 1, 2 * tpu, 3 * tpu - 1]
    pinv = np.stack([inv[:, bt * TT:(bt + 1) * TT] for bt in btiles]).astype(np.float32)
    return flag, mL, mR, pinv


def make_shared(inputs, depth):
    f = lambda a: np.ascontiguousarray(np.asarray(a, dtype=np.float32))
    L = depth
    sh = {
        "w_in": f(inputs["w_in"])[:L], "w_br_a": f(inputs["w_br_a"])[:L], "w_br_b": f(inputs["w_br_b"])[:L],
        "w_br_c": f(inputs["w_br_c"])[:L], "w_out": f(inputs["w_out"])[:L], "pool_w": f(inputs["pool_w"])[:L],
        "sgu_wT": np.ascontiguousarray(f(inputs["sgu_w"])[:L].transpose(0, 1, 3, 2)),
        "norm_g": f(inputs["norm_g"])[:L], "final_g": f(inputs["final_g"]).reshape(1, D),
        "pscale_t": np.ascontiguousarray(f(inputs["pool_scale"])[:L].reshape(L, 4, 128).transpose(0, 2, 1)),
        "lng_t": np.ascontiguousarray(f(inputs["sgu_ln_g"])[:L].reshape(L, 4, 128).transpose(0, 2, 1)),
        "sgu_ln_b": f(inputs["sgu_ln_b"])[:L], "sgu_b": f(inputs["sgu_b"])[:L].reshape(L, 512),
        "rel_bias": f(inputs["rel_bias"]),
    }
    return sh


_NC_CACHE = {}


def kernel(**inputs):
    unit, depth = 4096, 4
    xp = np.asarray(inputs["x_prompt"], dtype=np.float32)
    xs = np.asarray(inputs["x_sample"], dtype=np.float32)
    ident, anti, oh, valid = host_consts(unit)
    sh = make_shared(inputs, depth)
    sh.update({"ident": ident, "anti": anti, "oh": oh, "valid": valid})
    sh.update(swap_consts())
    in_maps = []
    for c in range(8):
        x = np.zeros((3 * unit, D), np.float32)
        if c < 2:
            x[0:2 * unit] = xp[c]
        else:
            x[0:unit] = xs[2 * (c - 2)]
            x[unit:2 * unit] = xs[2 * (c - 2) + 1]
        if c < 4:
            x[2 * unit:] = xs[12 + c]
        flag, mL, mR, pinv = core_consts(unit, c < 2)
        m = dict(sh)
        m.update({"x": x, "flag": flag, "mL": mL, "mR": mR, "pinv": pinv})
        in_maps.append(m)
    if "nc" not in _NC_CACHE:
        _NC_CACHE["nc"] = build(Cfg(unit=unit, depth=depth))[0]
    nc = _NC_CACHE["nc"]
    res = run_bass_kernel_spmd(nc, in_maps, core_ids=list(range(8)))
    yp = np.zeros_like(xp)
    ys = np.zeros_like(xs)
    for c in range(8):
        y = np.asarray(res.results[c]["y"], dtype=np.float32)
        if c < 2:
            yp[c] = y[0:2 * unit]
        else:
            ys[2 * (c - 2)] = y[0:unit]
            ys[2 * (c - 2) + 1] = y[unit:2 * unit]
        if c < 4:
            ys[12 + c] = y[2 * unit:]
    return (yp, ys)
```
